# Optimizing a Trainium2 kernel written in Bass

```python
import math
import jax, jax.numpy as jnp
from jax import lax
import numpy as np

D_MODEL = 1024
BATCH = 4
SEQ = 8192
DEPTH = 4

D_FF = 2816
D_POOL = 256
POOL_WINDOWS = (2, 4, 8, 16)
POOL_GROUP = D_POOL // len(POOL_WINDOWS)
D_CONV = 256
CONV_WIDTH = 3
NA_HEADS = 8
NA_HEAD_DIM = 64
D_NA = NA_HEADS * NA_HEAD_DIM
D_MIX = D_POOL + D_CONV + D_NA
GRID_W = 64
NA_ROWS = 8
NA_COLS = 16
D_IN = D_POOL + 3 * D_CONV + 3 * D_NA
ALPHA = (2.0 * DEPTH) ** 0.25
BETA = (8.0 * DEPTH) ** -0.25
LN_EPS = 1e-5
NEG_INF = -1e30

kernel_name = "hybrid_pool_conv_natten_encoder"


def layer_norm(x, g, b):
    xf = x.astype(jnp.float32)
    mu = jnp.mean(xf, axis=-1, keepdims=True)
    var = jnp.mean(jnp.square(xf - mu), axis=-1, keepdims=True)
    y = (xf - mu) * lax.rsqrt(var + LN_EPS)
    return (y * g.astype(jnp.float32) + b.astype(jnp.float32)).astype(x.dtype)


def swiglu(x, w_gate, w_up, w_down):
    return (jax.nn.silu(x @ w_gate) * (x @ w_up)) @ w_down


def pool_mixer(u, pool_w, pool_scale):
    bsz, s, _ = u.shape
    ng = len(POOL_WINDOWS)
    uf = u.astype(jnp.float32).reshape(bsz, s, ng, POOL_GROUP)
    cs = jnp.concatenate([jnp.zeros((bsz, 1, ng, POOL_GROUP), jnp.float32),
                          jnp.cumsum(uf, axis=1)], axis=1)
    t = jnp.arange(s)
    outs = []
    for g, w in enumerate(POOL_WINDOWS):
        lo = jnp.clip(t - w // 2, 0, s)
        hi = jnp.clip(t - w // 2 + w, 0, s)
        cnt = (hi - lo).astype(jnp.float32)[None, :, None]
        mean = (cs[:, hi, g] - cs[:, lo, g]) / cnt
        outs.append(mean - uf[:, :, g])
    p = jnp.stack(outs, axis=2).astype(u.dtype)
    y = jnp.einsum('bsgc,gcd->bsgd', p, pool_w) * pool_scale.reshape(ng, POOL_GROUP)
    return y.reshape(bsz, s, D_POOL)


def gated_conv_mixer(gate_b, gate_c, h, conv_w):
    z = gate_c * h
    zp = jnp.pad(z, ((0, 0), (1, 1), (0, 0)))
    y = conv_w[0] * zp[:, :-2] + conv_w[1] * zp[:, 1:-1] + conv_w[2] * zp[:, 2:]
    return gate_b * y


def neighbourhood_attention(q, k, v, rpb):
    bsz, s, _ = q.shape
    rows = s // GRID_W
    kr = min(NA_ROWS, rows)
    shp = (bsz, rows, GRID_W, NA_HEADS, NA_HEAD_DIM)
    q, k, v = q.reshape(shp), k.reshape(shp), v.reshape(shp)
    r = jnp.arange(rows)
    row_start = jnp.clip(r - kr // 2, 0, rows - kr)
    row_idx = row_start[:, None] + jnp.arange(kr)[None, :]
    kb = k[:, row_idx]
    vb = v[:, row_idx]
    c = jnp.arange(GRID_W)
    col_start = jnp.clip(c - NA_COLS // 2, 0, GRID_W - NA_COLS)
    col_valid = (c[None, :] >= col_start[:, None]) & (c[None, :] < col_start[:, None] + NA_COLS)
    dr = row_idx - r[:, None] + (NA_ROWS - 1)
    dc = jnp.clip(c[None, :] - c[:, None], -(NA_COLS - 1), NA_COLS - 1) + (NA_COLS - 1)
    bias = rpb[:, dr[:, None, :, None], dc[None, :, None, :]]
    scores = jnp.einsum('brqhd,brikhd->bhrqik', q, kb).astype(jnp.float32) * (NA_HEAD_DIM ** -0.5)
    scores = scores + bias.astype(jnp.float32)
    scores = jnp.where(col_valid[:, None, :], scores, NEG_INF)
    p = jax.nn.softmax(scores, axis=(-2, -1)).astype(v.dtype)
    o = jnp.einsum('bhrqik,brikhd->brqhd', p, vb)
    return o.reshape(bsz, s, D_NA)


def setup_inputs(seed: int = 0) -> dict:
    key = jax.random.key(seed)
    ks = jax.random.split(key, 16)
    L, D, F = DEPTH, D_MODEL, D_FF
    nrm = lambda k, shp: jax.random.normal(k, shp, jnp.float32)
    x = nrm(ks[0], (BATCH, SEQ, D))
    ffn1_w_gate = nrm(ks[1], (L, D, F)) * D ** -0.5
    ffn1_w_up = nrm(ks[2], (L, D, F)) * D ** -0.5
    ffn1_w_down = nrm(ks[3], (L, F, D)) * (BETA * F ** -0.5)
    ffn2_w_gate = nrm(ks[4], (L, D, F)) * D ** -0.5
    ffn2_w_up = nrm(ks[5], (L, D, F)) * D ** -0.5
    ffn2_w_down = nrm(ks[6], (L, F, D)) * (BETA * F ** -0.5)
    col_scale = jnp.concatenate([
        jnp.ones((D_POOL + 2 * D_CONV,), jnp.float32),
        jnp.full((D_CONV,), BETA, jnp.float32),
        jnp.ones((2 * D_NA,), jnp.float32),
        jnp.full((D_NA,), BETA, jnp.float32)])
    w_in = nrm(ks[7], (L, D, D_IN)) * D ** -0.5 * col_scale
    pool_w = nrm(ks[8], (L, len(POOL_WINDOWS), POOL_GROUP, POOL_GROUP)) * POOL_GROUP ** -0.5
    pool_scale = 1.0 + 0.1 * nrm(ks[9], (L, D_POOL))
    conv_w = nrm(ks[10], (L, CONV_WIDTH, D_CONV)) * CONV_WIDTH ** -0.5
    rpb = 0.02 * nrm(ks[11], (L, NA_HEADS, 2 * NA_ROWS - 1, 2 * NA_COLS - 1))
    w_out = nrm(ks[12], (L, D_MIX, D)) * (BETA * D_MIX ** -0.5)
    ln_g = 1.0 + 0.05 * nrm(ks[13], (L, 3, D))
    ln_b = 0.02 * nrm(ks[14], (L, 3, D))
    return {"x": x,
            "ffn1_w_gate": ffn1_w_gate, "ffn1_w_up": ffn1_w_up, "ffn1_w_down": ffn1_w_down,
            "ffn2_w_gate": ffn2_w_gate, "ffn2_w_up": ffn2_w_up, "ffn2_w_down": ffn2_w_down,
            "w_in": w_in, "pool_w": pool_w, "pool_scale": pool_scale, "conv_w": conv_w,
            "rpb": rpb, "w_out": w_out, "ln_g": ln_g, "ln_b": ln_b}


def reference(x, ffn1_w_gate, ffn1_w_up, ffn1_w_down, ffn2_w_gate, ffn2_w_up, ffn2_w_down,
              w_in, pool_w, pool_scale, conv_w, rpb, w_out, ln_g, ln_b):
    splits = np.cumsum([D_POOL, D_CONV, D_CONV, D_CONV, D_NA, D_NA])
    for l in range(DEPTH):
        x = layer_norm(ALPHA * x + 0.5 * swiglu(x, ffn1_w_gate[l], ffn1_w_up[l], ffn1_w_down[l]),
                       ln_g[l, 0], ln_b[l, 0])
        proj = x @ w_in[l]
        u, gb, gc, h, q, k, v = jnp.split(proj, splits, axis=-1)
        y_a = pool_mixer(u, pool_w[l], pool_scale[l])
        y_b = gated_conv_mixer(gb, gc, h, conv_w[l])
        y_c = neighbourhood_attention(q, k, v, rpb[l])
        y = jnp.concatenate([y_a, y_b, y_c], axis=-1) @ w_out[l]
        x = layer_norm(ALPHA * x + y, ln_g[l, 1], ln_b[l, 1])
        x = layer_norm(ALPHA * x + 0.5 * swiglu(x, ffn2_w_gate[l], ffn2_w_up[l], ffn2_w_down[l]),
                       ln_g[l, 2], ln_b[l, 2])
    return x
```

```python
import contextlib
import numpy as np
import concourse.bass as bass
import concourse.mybir as mybir
from concourse.bass_utils import run_bass_kernel_spmd

F32 = mybir.dt.float32
BF16 = mybir.dt.bfloat16
AF = mybir.ActivationFunctionType
ALU = mybir.AluOpType

D = 1024
DFF = 2816
NFC = DFF // 128
DIN = 2560
GRID_W = 64
ALPHA = 8.0 ** 0.25
LN_EPS = 1e-5
NEG = -30000.0

ENGS = ("pe", "act", "dve", "pool", "sp")
SEM_CAP = 30000
NDMA = 24


class Res:
    __slots__ = ("w", "rc", "rd")

    def __init__(self):
        self.w = None
        self.rc = {}
        self.rd = []


class Op:
    __slots__ = ("eng", "fn", "deps", "dma", "sem", "val", "marked", "prev")

    def __init__(self, eng, fn, dma):
        self.eng = eng
        self.fn = fn
        self.deps = []
        self.dma = dma
        self.sem = None
        self.val = 0
        self.marked = dma
        self.prev = None


class Prog:
    def __init__(self, nc):
        self.nc = nc
        self.ops = {e: [] for e in ENGS}

    def op(self, eng, fn, reads=(), writes=(), dma=False):
        o = Op(eng, fn, dma)
        deps = {}
        for r in reads:
            if r.w is not None:
                deps[id(r.w)] = r.w
        for r in writes:
            if r.w is not None:
                deps[id(r.w)] = r.w
            for x in r.rc.values():
                deps[id(x)] = x
            for x in r.rd:
                deps[id(x)] = x
        for d in deps.values():
            if d is o:
                continue
            if d.eng == "pe" and eng == "pe" and not d.dma and not dma:
                continue
            d.marked = True
            o.deps.append(d)
        for r in reads:
            if dma:
                r.rd.append(o)
            else:
                r.rc[eng] = o
        for r in writes:
            r.w = o
            r.rc = {}
            r.rd = []
        self.ops[eng].append(o)
        return o

    def dma(self, eng, out, in_, reads, writes, slow=False):
        if slow:
            return self.op(eng, lambda e: e.dma_start(out=out, in_=in_, allow_slow_non_contiguous=True), reads, writes, dma=True)
        return self.op(eng, lambda e: e.dma_start(out=out, in_=in_), reads, writes, dma=True)

    def emit(self, final_waits=()):
        nc = self.nc
        with contextlib.ExitStack() as st:
            def new_sem(name):
                return st.enter_context(nc.semaphore(name))
            for e in ENGS:
                cnt, cur, k = 0, None, 0
                pool, i = [], 0
                for o in self.ops[e]:
                    if o.dma:
                        if len(pool) < NDMA:
                            pool.append([new_sem(f"d_{e}_{len(pool)}"), 0])
                        slot = pool[i % NDMA]
                        i += 1
                        o.prev = (slot[0], slot[1])
                        slot[1] += 16
                        o.sem, o.val = slot[0], slot[1]
                    elif o.marked:
                        if cur is None or cnt >= SEM_CAP:
                            cur = new_sem(f"s_{e}_{k}")
                            k += 1
                            cnt = 0
                        cnt += 1
                        o.sem, o.val = cur, cnt
            block = st.enter_context(nc.Block())

            def run(e, eng):
                waited = {}
                for o in self.ops[e]:
                    need = {}
                    for d in o.deps:
                        key = id(d.sem)
                        if waited.get(key, 0) >= d.val:
                            continue
                        if key not in need or need[key][1] < d.val:
                            need[key] = (d.sem, d.val)
                    if o.dma and o.prev[1] > 0:
                        key = id(o.prev[0])
                        if waited.get(key, 0) < o.prev[1] and (key not in need or need[key][1] < o.prev[1]):
                            need[key] = o.prev
                    for key, (s, v) in need.items():
                        eng.wait_ge(s, v)
                        waited[key] = v
                    ins = o.fn(eng)
                    if o.marked:
                        ins.then_inc(o.sem, 16 if o.dma else 1)
                if e == "sp":
                    for d in final_waits:
                        eng.wait_ge(d.sem, d.val)

            block.tensor(lambda eng: run("pe", eng))
            block.scalar(lambda eng: run("act", eng))
            block.vector(lambda eng: run("dve", eng))
            block.gpsimd(lambda eng: run("pool", eng))
            block.sync(lambda eng: run("sp", eng))


def build_program(NL, NB):
    NT = 4 * NB
    NTOK = NT * 128
    KS_MAX = NT - 5
    W0_MAX = NT - 8
    nc = bass.Bass("TRN2", target_bir_lowering=False)

    def din(name, shape, dt=F32):
        return nc.dram_tensor(name, list(shape), dt, kind="ExternalInput").ap()

    def dscr(name, shape, dt):
        return nc.dram_tensor(name, list(shape), dt, kind="Internal").ap()

    x_in = din("x", [NTOK, D])
    w_f32 = {
        "g1": din("ffn1_w_gate", [NL, D, DFF]), "u1": din("ffn1_w_up", [NL, D, DFF]), "d1": din("ffn1_w_down", [NL, DFF, D]),
        "g2": din("ffn2_w_gate", [NL, D, DFF]), "u2": din("ffn2_w_up", [NL, D, DFF]), "d2": din("ffn2_w_down", [NL, DFF, D]),
        "win": din("w_in", [NL, D, DIN]), "wout": din("w_out", [NL, D, D]),
    }
    pool_w_in = din("pool_w", [NL, 4, 64, 64])
    pool_scale_in = din("pool_scale", [NL, 256])
    conv_w_in = din("conv_w", [NL, 3, 256])
    ln_g_in = din("ln_g", [NL, 3, D])
    ln_b_in = din("ln_b", [NL, 3, D])
    biasg_in = din("bias_g", [NL, 5, 128, 5, 8 * 128])
    invcnt_in = din("invcnt", [2, 128, NTOK])
    out = nc.dram_tensor("out", [NTOK, D], F32, kind="ExternalOutput").ap()

    w_bf = {k: dscr("bf_" + k, v.shape, BF16) for k, v in w_f32.items()}
    X1s = dscr("x1s", [NTOK, D], F32)
    UTs = [dscr(f"uts{i}", [256, NTOK + 16], F32) for i in range(2)]
    ZTs = [dscr(f"zts{i}", [256, NTOK + 2], F32) for i in range(2)]
    GBTs = dscr("gbts", [256, NTOK], F32)
    QTs = dscr("qts", [512, NTOK], BF16)
    KTs = [dscr(f"kts{i}", [512, NTOK], BF16) for i in range(2)]
    VAs = [dscr(f"vas{i}", [NTOK, 8 * 65], BF16) for i in range(2)]

    P = Prog(nc)
    with contextlib.ExitStack() as st:
        def T(name, shape, dt):
            return st.enter_context(nc.sbuf_tensor(name, list(shape), dt))

        X = T("X", [128, 4, D], F32)
        XM = T("XM", [128, 2, D], F32)
        XT = T("XT", [128, 8, 512], BF16)
        YT = T("YT", [128, 8, 512], BF16)
        GT = T("GT", [128, NFC, 512], BF16)
        SIL = T("SIL", [128, 2, 512], F32)
        NSLOT = 6
        RING = T("RING", [128, NSLOT, 8 * 512], BF16)
        LNP = T("LNP", [128, 2, 2, D], F32)
        KTW = T("KTW", [128, 4, 1024], BF16)
        VAW = T("VAW", [128, 8, 8 * 65], BF16)
        QM = T("QM", [128, 2, 4, 512], BF16)
        BT = T("BT", [128, 5, 8 * 128], BF16)
        PT = T("PT", [128, 2, 10 * 128], BF16)
        YC = T("YC", [128, 2, 512], BF16)
        IDF = T("IDF", [128, 128], F32)
        IDB = T("IDB", [128, 128], BF16)
        SH = T("SH", [128, 4, 528], F32)
        UB = SH[:, 0, :]
        SCR = T("SCR", [128, 2, 528], F32)
        SA = SCR[:, 0, :]
        SB = SCR[:, 1, :]
        ICN = SH[:, 2, 0:512]
        PTP = T("PTP", [128, 2, 512], BF16)
        ZB = SH[:, 1, 0:514]
        GBB = SH[:, 3, 0:512]
        CY = T("CY", [128, 2, 512], F32)
        PW = T("PW", [128, 2, 2, 128], BF16)
        PSC = T("PSC", [128, NL, 2], F32)
        CW = T("CW", [128, NL, 3, 2], F32)
        STF = T("STF", [128, 2, 512], F32)
        STB = T("STB", [128, 4, 512], BF16)
        VST = T("VST", [128, 3, 8 * 65], BF16)
        GCB = SCR[:, :, 0:512]
        ST6 = T("ST6", [128, 2, 2, 6], F32)
        MV = T("MV", [128, 2, 4], F32)
        RCP = T("RCP", [128, 2, 8], F32)
        ZERO = T("ZERO", [128, 16], F32)
        PSUM = st.enter_context(nc.psum_tensor("PSUM", [128, 8, 512], F32))

        rX = [[Res() for _ in range(2)] for _ in range(4)]
        rXM = [Res() for _ in range(2)]
        rXT = [Res() for _ in range(4)]
        rYT = [[Res() for _ in range(4)] for _ in range(8)]
        rGT = [Res() for _ in range(NFC)]
        rSIL = [Res() for _ in range(2)]
        rRING = [Res() for _ in range(NSLOT)]
        rLNP = [[Res(), Res()] for _ in range(2)]
        rKTW, rVAW, rBT = Res(), Res(), Res()
        rQM = [Res(), Res()]
        rPT = [Res() for _ in range(2)]
        rYC = [Res() for _ in range(2)]
        rID = Res()
        rSH = [Res() for _ in range(4)]
        rUB, rZB, rICN, rGBB = rSH
        rSA, rSB = Res(), Res()
        rCY = [Res(), Res()]
        rPTP = [Res() for _ in range(2)]
        rPW = [Res() for _ in range(2)]
        rSMALL = Res()
        rSTF = [Res() for _ in range(2)]
        rSTB = [Res() for _ in range(4)]
        rVST = [Res() for _ in range(3)]
        rGCB = [rSA, rSB]
        rST = [Res() for _ in range(2)]
        rRCP = [Res() for _ in range(2)]
        rZERO = Res()
        rBANK = [Res() for _ in range(8)]
        rX1s = [Res() for _ in range(NB)]
        rUTs = [[[Res() for _ in range(2)] for _ in range(NB)] for _ in range(2)]
        rZTs = [[[Res() for _ in range(2)] for _ in range(NB)] for _ in range(2)]
        rGBTs = [[Res() for _ in range(2)] for _ in range(NB)]
        rQTs = [[Res() for _ in range(4)] for _ in range(NB)]
        rKTs = [[[Res() for _ in range(4)] for _ in range(NB)] for _ in range(2)]
        rVAs = [[[Res() for _ in range(4)] for _ in range(NB)] for _ in range(2)]
        rPAD = Res()
        rWbf = {}

        state = {"bank": 0, "stf": 0, "stb": 0, "vst": 0, "sil": 0, "lnp": 0, "slot": 0, "xm": 0, "bst": 0,
                 "pt": 0, "yc": 0, "st": 0, "rcp": 0, "gcb": 0, "ptp": 0}

        def rot(key, n):
            i = state[key]
            state[key] = (i + 1) % n
            return i

        def bank():
            i = rot("bank", 6)
            return PSUM[:, i, :], rBANK[i]

        def mm(o, lhsT, rhs, start, stop, reads, writes, skip=False):
            if skip:
                P.op("pe", lambda e: e.matmul(o, lhsT, rhs, start=start, stop=stop, skip_group_check=True), reads, writes)
            else:
                P.op("pe", lambda e: e.matmul(o, lhsT, rhs, start=start, stop=stop), reads, writes)

        P.op("pool", lambda e: e.memset(IDF[:], 0.0), [], [rID])
        P.op("pool", lambda e: e.affine_select(out=IDF[:], in_=IDF[:], pattern=[[-1, 128]], compare_op=ALU.not_equal,
                                               fill=1.0, base=0, channel_multiplier=1), [rID], [rID])
        P.op("pool", lambda e: e.tensor_copy(out=IDB[:], in_=IDF[:]), [rID], [rID])
        P.op("pool", lambda e: e.memset(ZERO[:], 0.0), [], [rZERO])
        P.op("pool", lambda e: e.memset(VST[:], 1.0), [], rVST)
        P.op("pool", lambda e: e.memset(PW[:], 0.0), [], [rPW[0], rPW[1]])
        P.op("pool", lambda e: e.memset(QM[:], 0.0), [], rQM)
        for i in range(2):
            for c in range(2):
                rows = slice(c * 128, (c + 1) * 128)
                P.dma("sp", UTs[i][rows, 0:8], ZERO[:, 0:8], [rZERO], [rPAD])
                P.dma("sp", UTs[i][rows, NTOK + 8:NTOK + 16], ZERO[:, 0:8], [rZERO], [rPAD])
                P.dma("sp", ZTs[i][rows, 0:1], ZERO[:, 0:1], [rZERO], [rPAD], slow=True)
                P.dma("sp", ZTs[i][rows, NTOK + 1:NTOK + 2], ZERO[:, 0:1], [rZERO], [rPAD], slow=True)
        for l in range(NL):
            P.dma("sp", PSC[:, l, :], pool_scale_in[l, :].rearrange("(c p) -> p c", p=128), [], [rSMALL], slow=True)
            for j in range(3):
                P.dma("sp", CW[:, l, j, :], conv_w_in[l, j, :].rearrange("(c p) -> p c", p=128), [], [rSMALL], slow=True)

        def panel_list(l):
            out_ = []
            def ffn(g, u, d):
                for pn in range(6):
                    out_.append((g, "col", pn)); out_.append((u, "col", pn))
                for h in range(2):
                    for kp in range(3):
                        out_.append((d, "blk", (kp, h)))
            return out_, ffn

        def conv_src_dst(key, l, kind, idx):
            src, dst = w_f32[key], w_bf[key]
            if kind == "col":
                c0 = idx * 512
                w = min(512, src.shape[2] - c0)
                return src[l, :, c0:c0 + w], dst[l, :, c0:c0 + w]
            kp, h = idx
            r0 = kp * 1024
            nr = min(1024, src.shape[1] - r0)
            return src[l, r0:r0 + nr, h * 512:(h + 1) * 512], dst[l, r0:r0 + nr, h * 512:(h + 1) * 512]

        conv_queue = []

        def queue_layer_conversions(l):
            pieces = []
            for pn in range(6):
                pieces += [("g1", l, "col", pn), ("u1", l, "col", pn)]
            for h in range(2):
                for kp in range(3):
                    pieces.append(("d1", l, "blk", (kp, h)))
            for pn in range(5):
                pieces.append(("win", l, "col", pn))
            for h in range(2):
                pieces.append(("wout", l, "col", h))
            for pn in range(6):
                pieces += [("g2", l, "col", pn), ("u2", l, "col", pn)]
            for h in range(2):
                for kp in range(3):
                    pieces.append(("d2", l, "blk", (kp, h)))
            conv_queue.extend(pieces)

        def pump_conversions(n):
            for _ in range(min(n, len(conv_queue))):
                key, l, kind, idx = conv_queue.pop(0)
                s, d = conv_src_dst(key, l, kind, idx)
                r = Res()
                rWbf[(key, l, kind, idx)] = r
                P.dma("pool", d, s, [], [r])

        def panel_sequence():
            seq = []
            def ffn_p(l, g, u, d):
                for pn in range(6):
                    seq.append((g, l, "col", pn)); seq.append((u, l, "col", pn))
                for h in range(2):
                    for kp in range(3):
                        seq.append((d, l, "blk", (kp, h)))
            for it in range(NL + 1):
                for b_ in range(NB):
                    if it >= 1:
                        seq.append(("wout", it - 1, "col", 0)); seq.append(("wout", it - 1, "col", 1))
                        ffn_p(it - 1, "g2", "u2", "d2")
                    if it < NL:
                        ffn_p(it, "g1", "u1", "d1")
                        for pn in range(5):
                            seq.append(("win", it, "col", pn))
            return seq

        pseq = panel_sequence()
        pst = {"issued": 0, "acq": 0}
        released = [False] * len(pseq)
        pinfo = {}

        def issue_panel(n):
            key, l, kind, idx = pseq[n]
            while (key, l, kind, idx) not in rWbf:
                pump_conversions(1)
            s = n % NSLOT
            src = w_bf[key]
            if kind == "col":
                c0 = idx * 512
                w = min(512, src.shape[2] - c0)
                sap = src[l, :, c0:c0 + w].rearrange("(k p) f -> p k f", p=128)
                dap = RING[:, s, 0:8 * w].rearrange("p (k f) -> p k f", k=8)
                P.dma("sp", dap, sap, [rWbf[(key, l, kind, idx)]], [rRING[s]])
                pinfo[n] = (s, w)
            else:
                kp, h = idx
                r0 = kp * 1024
                nr = min(1024, src.shape[1] - r0)
                nk = nr // 128
                sap = src[l, r0:r0 + nr, h * 512:(h + 1) * 512].rearrange("(k p) f -> p k f", p=128)
                dap = RING[:, s, 0:nk * 512].rearrange("p (k f) -> p k f", k=nk)
                P.dma("sp", dap, sap, [rWbf[(key, l, kind, idx)]], [rRING[s]])
                pinfo[n] = (s, nk)

        def pump_panels():
            while pst["issued"] < len(pseq) and (pst["issued"] < NSLOT or released[pst["issued"] - NSLOT]):
                issue_panel(pst["issued"])
                pst["issued"] += 1

        def load_panel(key, l, kind, idx):
            n = pst["acq"]
            assert pseq[n] == (key, l, kind, idx), (pseq[n], key, l, kind, idx)
            pump_panels()
            assert n < pst["issued"]
            pst["acq"] += 1
            s, w = pinfo[n]
            return n, s, w

        def release(*ns):
            for n in ns:
                released[n] = True
            pump_panels()

        def pview(s, w):
            return RING[:, s, 0:8 * w].rearrange("p (k f) -> p k f", k=8)

        def load_lnp(l, idx):
            i = rot("lnp", 2)
            P.dma("sp", LNP[:, i, 0, :], ln_g_in[l, idx, :].partition_broadcast(128), [], [rLNP[i][0]])
            P.dma("sp", LNP[:, i, 1, :], ln_b_in[l, idx, :].partition_broadcast(128), [], [rLNP[i][1]])
            return i

        def ln_tile(t, lnp_i, eps):
            si = rot("st", 2)
            xr = [rX[t][0], rX[t][1]]
            for h in range(2):
                P.op("dve", lambda e, h=h: e.bn_stats(out=ST6[:, si, h, :], in_=X[:, t, h * 512:(h + 1) * 512]),
                     [rX[t][h]], [rST[si]])
            P.op("dve", lambda e: e.bn_aggr(out=MV[:, si, 0:2], in_=ST6[:, si, :, :].rearrange("p a b -> p (a b)")),
                 [rST[si]], [rST[si]])
            P.op("act", lambda e: e.activation(out=MV[:, si, 2:3], in_=MV[:, si, 1:2], func=AF.Sqrt, bias=float(eps), scale=1.0),
                 [rST[si]], [rST[si]])
            P.op("dve", lambda e: e.reciprocal(out=MV[:, si, 2:3], in_=MV[:, si, 2:3]), [rST[si]], [rST[si]])
            for h in range(2):
                hs = slice(h * 512, (h + 1) * 512)
                P.op("dve", lambda e, hs=hs: e.scalar_tensor_tensor(out=X[:, t, hs], in0=X[:, t, hs], scalar=MV[:, si, 0:1],
                                                                    in1=LNP[:, lnp_i, 0, hs], op0=ALU.subtract, op1=ALU.mult),
                     [rST[si], rX[t][h], rLNP[lnp_i][0]], [rX[t][h]])
                P.op("dve", lambda e, hs=hs: e.scalar_tensor_tensor(out=X[:, t, hs], in0=X[:, t, hs], scalar=MV[:, si, 2:3],
                                                                    in1=LNP[:, lnp_i, 1, hs], op0=ALU.mult, op1=ALU.add),
                     [rST[si], rX[t][h], rLNP[lnp_i][1]], [rX[t][h]])

        def transpose_tile(t):
            for g in range(2):
                xr = [rX[t][g]]
                bk, rb = bank()
                for k in range(4):
                    kk = g * 4 + k
                    P.op("pe", lambda e, kk=kk, k=k, bk=bk: e.transpose(bk[:, k * 128:(k + 1) * 128], X[:, t, kk * 128:(kk + 1) * 128], IDF[:]),
                         xr + [rID], [rb])
                dst = XT[:, g * 4:(g + 1) * 4, t * 128:(t + 1) * 128]
                srcv = bk.rearrange("p (a b) -> p a b", a=4)
                if g == 0:
                    P.op("dve", lambda e, dst=dst, srcv=srcv: e.tensor_copy(out=dst, in_=srcv), [rb], [rXT[t]])
                else:
                    P.op("act", lambda e, dst=dst, srcv=srcv: e.activation(out=dst, in_=srcv, func=AF.Copy), [rb], [rXT[t]])

        def ffn(l, gk, uk, dk, ln_idx, last):
            lnp_i = load_lnp(l, ln_idx)
            pending = None
            for pn in range(6):
                ng, sg, w = load_panel(gk, l, "col", pn)
                nu, su, _ = load_panel(uk, l, "col", pn)
                pg, pu = pview(sg, w), pview(su, w)
                for c in range(w // 128):
                    fc = pn * 4 + c
                    bg, rbg = bank()
                    bu, rbu = bank()
                    for k in range(8):
                        mm(bg, pg[:, k, c * 128:(c + 1) * 128], XT[:, k, :], k == 0, k == 7, rXT + [rRING[sg]], [rbg])
                    for k in range(8):
                        mm(bu, pu[:, k, c * 128:(c + 1) * 128], XT[:, k, :], k == 0, k == 7, rXT + [rRING[su]], [rbu])
                    si = rot("sil", 2)
                    P.op("act", lambda e, si=si, bg=bg: e.activation(out=SIL[:, si, :], in_=bg, func=AF.Silu), [rbg], [rSIL[si]])
                    P.op("dve", lambda e, si=si, bu=bu, fc=fc: e.tensor_tensor(out=GT[:, fc, :], in0=SIL[:, si, :], in1=bu, op=ALU.mult),
                         [rSIL[si], rbu], [rGT[fc]])
                release(ng, nu)
            for h in range(2):
                slots = [load_panel(dk, l, "blk", (kp, h)) for kp in range(3)]
                for t in range(4):
                    by, rby = bank()
                    for fc in range(NFC):
                        _, s, nk = slots[fc // 8]
                        pv = RING[:, s, 0:nk * 512].rearrange("p (k f) -> p k f", k=nk)
                        mm(by, GT[:, fc, t * 128:(t + 1) * 128], pv[:, fc % 8, :], fc == 0, fc == NFC - 1,
                           [rGT[fc], rRING[s]], [rby])
                    P.op("dve", lambda e, t=t, h=h, by=by: e.scalar_tensor_tensor(
                        out=X[:, t, h * 512:(h + 1) * 512], in0=X[:, t, h * 512:(h + 1) * 512], scalar=2.0 * ALPHA,
                        in1=by, op0=ALU.mult, op1=ALU.add), [rX[t][h], rby], [rX[t][h]])
                    if h == 1:
                        ln_tile(t, lnp_i, 4.0 * LN_EPS)
                        if not last and t >= 1:
                            transpose_tile(t - 1)
                release(*[x[0] for x in slots])
            if not last:
                transpose_tile(3)

        def alt_view(k):
            return GT[:, k * 10:(k + 1) * 10, :].rearrange("p a b -> p (a b)").rearrange("p (i f) -> p i f", i=5)

        ALT = [alt_view(0), alt_view(1)]
        rALT = [rGT[0:10], rGT[10:20]]

        def prefetch_tables(l, b):
            if b == 0:
                P.dma("pool", BT[:], biasg_in[l, 0], [], [rBT])
                P.dma("pool", ALT[0], biasg_in[l, 1], [], rALT[0])
                P.dma("pool", ALT[1], biasg_in[l, 2], [], rALT[1])
            if b == NB - 1:
                P.dma("pool", ALT[0], biasg_in[l, 3], [], rALT[0])
                P.dma("pool", ALT[1], biasg_in[l, 4], [], rALT[1])


        def mixer_window(b):
            w0 = min(max(4 * b - 2, 0), W0_MAX)
            wblocks = sorted(set(min(max((w0 + i) // 4, 0), NB - 1) for i in range(8)))
            return w0, wblocks

        def mixer_chunk_loads(l, b, c):
            par = l % 2
            b0 = b * 512
            nbr = [bb for bb in (b - 1, b, b + 1) if 0 <= bb < NB]
            rows = slice(c * 128, (c + 1) * 128)
            P.dma("sp", UB[:], UTs[par][rows, b0:b0 + 528], [rUTs[par][bb][c] for bb in nbr] + [rPAD], [rUB])
            P.dma("sp", ICN[:], invcnt_in[c, :, b0:b0 + 512], [], [rICN])
            P.dma("sp", ZB[:], ZTs[par][rows, b0:b0 + 514], [rZTs[par][bb][c] for bb in nbr] + [rPAD], [rZB])
            P.dma("sp", GBB[:], GBTs[rows, b0:b0 + 512], [rGBTs[b][c]], [rGBB])

        def mixer_prefetch(l, b):
            par = l % 2
            b0 = b * 512
            w0, wblocks = mixer_window(b)
            mixer_chunk_loads(l, b, 0)
            qsrc = QTs[:, b0:b0 + 512].rearrange("(c p) t -> p c t", p=128)
            P.dma("sp", QM[0:64, 0, :, :], qsrc[0:64], rQTs[b], [rQM[0]])
            P.dma("sp", QM[64:128, 1, :, :], qsrc[64:128], rQTs[b], [rQM[1]])
            P.dma("sp", KTW[:], KTs[par][:, w0 * 128:w0 * 128 + 1024].rearrange("(c p) t -> p c t", p=128),
                  [r for bb in wblocks for r in rKTs[par][bb]], [rKTW])
            P.dma("sp", VAW[:], VAs[par][w0 * 128:w0 * 128 + 1024, :].rearrange("(t p) f -> p t f", p=128),
                  [r for bb in wblocks for r in rVAs[par][bb]], [rVAW])

        def mixer(l, b, prefetched):
            par = l % 2
            b0 = b * 512
            w0, wblocks = mixer_window(b)
            nbr = [bb for bb in (b - 1, b, b + 1) if 0 <= bb < NB]
            if b == 0:
                for g in range(4):
                    c, hh = g // 2, g % 2
                    P.dma("pool", PW[hh * 64:(hh + 1) * 64, par, c, hh * 64:(hh + 1) * 64], pool_w_in[l, g, :, :], [], [rPW[par]])
            if not prefetched:
                mixer_prefetch(l, b)
            lnp_i = load_lnp(l, 1)
            lo, hi = slice(0, 64), slice(64, 128)

            def add(o, a_, b_, rr, ww):
                P.op("pool", lambda e: e.tensor_tensor(out=o, in0=a_, in1=b_, op=ALU.add), rr, ww)

            def elementwise(c):
                add(SA[:, 1:528], UB[:, 1:528], UB[:, 0:527], [rUB], [rSA])
                if c == 0:
                    add(SB[hi, 3:528], SA[hi, 3:528], SA[hi, 1:526], [rSA], [rSB])
                    sel_lo, sel_hi = SA[lo, 8:520], SB[hi, 9:521]
                else:
                    add(SB[:, 3:528], SA[:, 3:528], SA[:, 1:526], [rSA], [rSB])
                    add(SA[:, 7:528], SB[:, 7:528], SB[:, 3:524], [rSB], [rSA])
                    add(SB[hi, 15:528], SA[hi, 15:528], SA[hi, 7:520], [rSA], [rSB])
                    sel_lo, sel_hi = SA[lo, 11:523], SB[hi, 15:527]
                for part, sel in ((lo, sel_lo), (hi, sel_hi)):
                    P.op("pool", lambda e, part=part, sel=sel: e.tensor_tensor(out=CY[part, 0, :], in0=sel, in1=ICN[part, :], op=ALU.mult),
                         [rSA, rSB, rICN], [rCY[0]])
                    P.op("pool", lambda e, part=part, c=c: e.tensor_tensor(out=PTP[part, c, :], in0=CY[part, 0, :], in1=UB[part, 8:520], op=ALU.subtract),
                         [rCY[0], rUB], [rPTP[c]])
                P.op("dve", lambda e, c=c: e.tensor_scalar(out=CY[:, 1, :], in0=ZB[:, 1:513], scalar1=CW[:, l, 1, c:c + 1], scalar2=None, op0=ALU.mult),
                     [rZB, rSMALL], [rCY[1]])
                P.op("dve", lambda e, c=c: e.scalar_tensor_tensor(out=CY[:, 1, :], in0=ZB[:, 0:512], scalar=CW[:, l, 0, c:c + 1], in1=CY[:, 1, :],
                                                                  op0=ALU.mult, op1=ALU.add), [rZB, rSMALL, rCY[1]], [rCY[1]])
                P.op("dve", lambda e, c=c: e.scalar_tensor_tensor(out=CY[:, 1, :], in0=ZB[:, 2:514], scalar=CW[:, l, 2, c:c + 1], in1=CY[:, 1, :],
                                                                  op0=ALU.mult, op1=ALU.add), [rZB, rSMALL, rCY[1]], [rCY[1]])
                P.op("dve", lambda e, c=c: e.tensor_tensor(out=YT[:, 2 + c, :], in0=CY[:, 1, :], in1=GBB[:], op=ALU.mult),
                     [rCY[1], rGBB], rYT[2 + c])

            elementwise(0)
            mixer_chunk_loads(l, b, 1)

            ob = [(PSUM[:, 6, :], rBANK[6]), (PSUM[:, 7, :], rBANK[7])]
            tile_ctx = {}

            def scores(t, hp):
                j = 4 * b + t
                if hp == 0:
                    ks = min(max(j - 2, 0), KS_MAX)
                    tile_ctx[t] = (ks - w0, rot("yc", 2), rot("rcp", 2))
                kw, yi, ri = tile_ctx[t]
                tid = {0: 1, 1: 2, NT - 2: 3, NT - 1: 4}.get(j, 0)
                if tid == 0:
                    BTv, rBTv = BT, [rBT]
                else:
                    BTv, rBTv = ALT[(tid - 1) % 2], rALT[(tid - 1) % 2]
                pi = rot("pt", 2)
                bA, rA = bank()
                bB, rB = bank()
                bC, rC = bank()
                hsl = slice(2 * hp * 128, (2 * hp + 2) * 128)
                mm(bA.rearrange("p (a b) -> p a b", a=2), IDB[:], BTv[:, 0:2, hsl], True, False, [rID] + rBTv, [rA], skip=True)
                mm(bB.rearrange("p (a b) -> p a b", a=2), IDB[:], BTv[:, 2:4, hsl], True, False, [rID] + rBTv, [rB], skip=True)
                mm(bC[:, 0:256], IDB[:], BTv[:, 4, hsl], True, False, [rID] + rBTv, [rC], skip=True)
                for i in range(5):
                    bk_, rr = ((bA, rA), (bB, rB), (bC, rC))[i // 2]
                    o = bk_[:, (i % 2) * 256:(i % 2 + 1) * 256].rearrange("p (a b) -> p a b", a=2)
                    mm(o, KTW[:, hp, (kw + i) * 128:(kw + i + 1) * 128], QM[:, :, hp, t * 128:(t + 1) * 128],
                       False, True, [rKTW] + rQM, [rr], skip=True)
                P.op("act", lambda e: e.activation(out=PT[:, pi, 0:512], in_=bA, func=AF.Exp), [rA], [rPT[pi]])
                P.op("act", lambda e: e.activation(out=PT[:, pi, 512:1024], in_=bB, func=AF.Exp), [rB], [rPT[pi]])
                P.op("act", lambda e: e.activation(out=PT[:, pi, 1024:1280], in_=bC[:, 0:256], func=AF.Exp), [rC], [rPT[pi]])

                def pv():
                    for hh in range(2):
                        h = 2 * hp + hh
                        obk, obr = ob[h // 4]
                        o = obk[:, (h % 4) * 65:(h % 4 + 1) * 65]
                        for i in range(5):
                            c0 = i * 256 + hh * 128
                            mm(o, PT[:, pi, c0:c0 + 128], VAW[:, kw + i, h * 65:(h + 1) * 65], i == 0, i == 4,
                               [rPT[pi], rVAW], [obr])
                    if hp % 2 == 1:
                        g = hp // 2
                        obk, obr = ob[g]
                        ov = obk[:, 0:260].rearrange("p (h d) -> p h d", h=4)
                        P.op("dve", lambda e: e.reciprocal(out=RCP[:, ri, g * 4:(g + 1) * 4], in_=ov[:, :, 64]), [obr], [rRCP[ri]])
                        for h4 in range(4):
                            h = g * 4 + h4
                            P.op("dve", lambda e, h4=h4, h=h: e.tensor_scalar(
                                out=YC[:, yi, h * 64:(h + 1) * 64], in0=ov[:, h4, 0:64], scalar1=RCP[:, ri, h:h + 1], scalar2=None, op0=ALU.mult),
                                [obr, rRCP[ri]], [rYC[yi]])
                    if hp != 3:
                        return None

                    def tr():
                        bk, rb = bank()
                        bkb = bk.bitcast(BF16)
                        for c in range(4):
                            P.op("pe", lambda e, c=c: e.transpose(bkb[:, c * 128:(c + 1) * 128], YC[:, yi, c * 128:(c + 1) * 128], IDB[:]),
                                 [rYC[yi], rID], [rb])
                        P.op("act", lambda e: e.activation(out=YT[:, 4:8, t * 128:(t + 1) * 128],
                                                           in_=bkb[:, 0:512].rearrange("p (a b) -> p a b", a=4), func=AF.Copy),
                             [rb], [rYT[4 + c][t] for c in range(4)])
                    return tr
                return pv

            queue = []
            n = 0
            for t in range(4):
                for hp in range(4):
                    pvf = scores(t, hp)
                    due = [f for (d, f) in queue if d <= n]
                    queue = [(d, f) for (d, f) in queue if d > n]
                    for f in due:
                        r_ = f()
                        if r_ is not None:
                            queue.append((n + 1, r_))
                    queue.append((n + 1, pvf))
                    n += 1
                if t == 1:
                    elementwise(1)
            while queue:
                d, f = queue.pop(0)
                r_ = f()
                if r_ is not None:
                    queue.append((d + 1, r_))

            for c in range(2):
                bk, rb = bank()
                mm(bk, PW[:, par, c, :], PTP[:, c, :], True, True, [rPW[par], rPTP[c]], [rb])
                P.op("act", lambda e, c=c, bk=bk: e.activation(out=YT[:, c, :], in_=bk, func=AF.Copy, scale=PSC[:, l, c:c + 1]),
                     [rb, rSMALL], rYT[c])

            panels = [load_panel("wout", l, "col", h) for h in range(2)]
            for t in range(4):
                xi = rot("xm", 2)
                P.dma("sp", XM[:, xi, :], X1s[b0 + t * 128:b0 + (t + 1) * 128, :], [rX1s[b]], [rXM[xi]])
                for h in range(2):
                    _, s_, w = panels[h]
                    pv_ = pview(s_, w)
                    by, rby = bank()
                    for k in range(8):
                        mm(by, YT[:, k, t * 128:(t + 1) * 128], pv_[:, k, :], k == 0, k == 7,
                           [rYT[k][tt] for tt in range(4)] + [rRING[s_]], [rby])
                    P.op("dve", lambda e, t=t, h=h, by=by, xi=xi: e.scalar_tensor_tensor(
                        out=X[:, t, h * 512:(h + 1) * 512], in0=XM[:, xi, h * 512:(h + 1) * 512], scalar=ALPHA,
                        in1=by, op0=ALU.mult, op1=ALU.add), [rXM[xi], rby], [rX[t][h]])
                ln_tile(t, lnp_i, LN_EPS)
                if t >= 1:
                    transpose_tile(t - 1)
            release(panels[0][0], panels[1][0])
            transpose_tile(3)

        SQ = "act"

        def project(l, b):
            par = l % 2
            b0 = b * 512
            P.dma(SQ, X1s[b0:b0 + 512, :].rearrange("(t p) d -> p t d", p=128), X[:],
                  [rX[t][h] for t in range(4) for h in range(2)], [rX1s[b]])
            for pn in range(4):
                n_, s, w = load_panel("win", l, "col", pn)
                pv = pview(s, w)
                for c in range(4):
                    bk, rb = bank()
                    for k in range(8):
                        mm(bk, pv[:, k, c * 128:(c + 1) * 128], XT[:, k, :], k == 0, k == 7, rXT + [rRING[s]], [rb])
                    if pn == 0:
                        i = rot("stf", 2)
                        if c < 2:
                            P.op("act", lambda e, i=i, bk=bk: e.activation(out=STF[:, i, :], in_=bk, func=AF.Copy), [rb], [rSTF[i]])
                            P.dma(SQ, UTs[par][c * 128:(c + 1) * 128, 8 + b0:8 + b0 + 512], STF[:, i, :], [rSTF[i]], [rUTs[par][b][c]])
                        else:
                            P.op("dve", lambda e, i=i, bk=bk: e.tensor_copy(out=STF[:, i, :], in_=bk), [rb], [rSTF[i]])
                            P.dma(SQ, GBTs[(c - 2) * 128:(c - 1) * 128, b0:b0 + 512], STF[:, i, :], [rSTF[i]], [rGBTs[b][c - 2]])
                    elif pn == 1:
                        if c < 2:
                            gi = c
                            P.op("act", lambda e, gi=gi, bk=bk: e.activation(out=GCB[:, gi, :], in_=bk, func=AF.Copy), [rb], [rGCB[gi]])
                        else:
                            gi = c - 2
                            i = rot("stf", 2)
                            P.op("dve", lambda e, i=i, gi=gi, bk=bk: e.tensor_tensor(out=STF[:, i, :], in0=GCB[:, gi, :], in1=bk, op=ALU.mult),
                                 [rb, rGCB[gi]], [rSTF[i]])
                            P.dma(SQ, ZTs[par][gi * 128:(gi + 1) * 128, 1 + b0:1 + b0 + 512], STF[:, i, :], [rSTF[i]], [rZTs[par][b][gi]])
                    elif pn == 2:
                        i = rot("stb", 4)
                        P.op("act", lambda e, i=i, bk=bk: e.activation(out=STB[:, i, :], in_=bk, func=AF.Copy, scale=0.125), [rb], [rSTB[i]])
                        P.dma(SQ, QTs[c * 128:(c + 1) * 128, b0:b0 + 512], STB[:, i, :], [rSTB[i]], [rQTs[b][c]])
                    else:
                        i = rot("stb", 4)
                        P.op("dve", lambda e, i=i, bk=bk: e.tensor_copy(out=STB[:, i, :], in_=bk), [rb], [rSTB[i]])
                        P.dma(SQ, KTs[par][c * 128:(c + 1) * 128, b0:b0 + 512], STB[:, i, :], [rSTB[i]], [rKTs[par][b][c]])
                release(n_)
            n_, s, w = load_panel("win", l, "col", 4)
            pv = pview(s, w)
            for t in range(4):
                bk, rb = bank()
                for k in range(8):
                    mm(bk, XT[:, k, t * 128:(t + 1) * 128], pv[:, k, :], k == 0, k == 7, [rXT[t], rRING[s]], [rb])
                vi = rot("vst", 3)
                dst = VST[:, vi, :].rearrange("p (h d) -> p h d", h=8)[:, :, 0:64]
                srcv = bk.rearrange("p (h d) -> p h d", h=8)
                if t % 2 == 0:
                    P.op("act", lambda e, dst=dst, srcv=srcv: e.activation(out=dst, in_=srcv, func=AF.Copy), [rb], [rVST[vi]])
                else:
                    P.op("dve", lambda e, dst=dst, srcv=srcv: e.tensor_copy(out=dst, in_=srcv), [rb], [rVST[vi]])
                P.dma(SQ, VAs[par][b0 + t * 128:b0 + (t + 1) * 128, :], VST[:, vi, :], [rVST[vi]], [rVAs[par][b][t]])
            release(n_)

        queue_layer_conversions(0)
        pump_conversions(len(conv_queue))
        out_stores = []
        for it in range(NL + 1):
            if it + 1 < NL:
                queue_layer_conversions(it + 1)
            per_block = (len(conv_queue) + NB - 1) // NB if conv_queue else 0
            for b in range(NB):
                b0 = b * 512
                if it >= 1:
                    mixer(it - 1, b, prefetched=(b >= 1))
                    ffn(it - 1, "g2", "u2", "d2", 2, last=(it == NL))
                else:
                    P.dma("sp", X[:], x_in[b0:b0 + 512, :].rearrange("(t p) d -> p t d", p=128), [],
                          [rX[t][h] for t in range(4) for h in range(2)])
                    for t in range(4):
                        transpose_tile(t)
                if it == NL:
                    out_stores.append(P.dma("sp", out[b0:b0 + 512, :].rearrange("(t p) d -> p t d", p=128), X[:],
                                            [rX[t][h] for t in range(4) for h in range(2)], []))
                    if b + 1 < NB:
                        mixer_prefetch(it - 1, b + 1)
                        prefetch_tables(it - 1, b + 1)
                else:
                    if it >= 1 and b + 1 < NB:
                        mixer_prefetch(it - 1, b + 1)
                    ffn(it, "g1", "u1", "d1", 0, last=False)
                    if b + 1 < NB:
                        if it >= 1:
                            prefetch_tables(it - 1, b + 1)
                    else:
                        prefetch_tables(it, 0)
                    project(it, b)
                pump_conversions(per_block)
        P.emit(final_waits=out_stores)
    return nc


def _tables(NT):
    R = 2 * NT
    KS_MAX = NT - 5
    tiles = [min(10, NT // 2), 0, 1, NT - 2, NT - 1]
    dr_idx = np.zeros((5, 128, 5, 128), np.int64)
    dc_idx = np.zeros((5, 128, 5, 128), np.int64)
    valid = np.zeros((5, 128, 5, 128), bool)
    kk = np.arange(128)[:, None]
    qq = np.arange(128)[None, :]
    for ti, j in enumerate(tiles):
        ks = min(max(j - 2, 0), KS_MAX)
        for i in range(5):
            kt = ks + i
            krow = 2 * kt + kk // 64
            kcol = kk % 64
            qrow = 2 * j + qq // 64
            qcol = qq % 64
            rs = np.clip(qrow - 4, 0, R - 8)
            cs = np.clip(qcol - 8, 0, GRID_W - 16)
            v = (krow >= rs) & (krow < rs + 8) & (kcol >= cs) & (kcol < cs + 16)
            dr = np.clip(krow - qrow + 7, 0, 14)
            dc = np.clip(kcol - qcol, -15, 15) + 15
            dr_idx[ti, :, i, :] = np.broadcast_to(dr, (128, 128))
            dc_idx[ti, :, i, :] = np.broadcast_to(dc, (128, 128))
            valid[ti, :, i, :] = v
    return dr_idx, dc_idx, valid


def _invcnt(NTOK, off, S):
    t = np.arange(NTOK) + off
    res = np.zeros((2, 128, NTOK), np.float32)
    for g, w in enumerate((2, 4, 8, 16)):
        lo = np.clip(t - w // 2, 0, S)
        hi = np.clip(t - w // 2 + w, 0, S)
        cnt = np.maximum(hi - lo, 1).astype(np.float32)
        c, hh = g // 2, g % 2
        res[c, hh * 64:(hh + 1) * 64, :] = (1.0 / cnt)[None, :]
    return res


_CACHE = {}


def run_cores(xs, offs, S, params, NL, NB):
    NT = 4 * NB
    NTOK = NT * 128
    key = (NL, NB)
    if key not in _CACHE:
        _CACHE[key] = build_program(NL, NB)
    nc = _CACHE[key]
    dr_idx, dc_idx, valid = _tables(NT)
    rpb = np.asarray(params["rpb"], np.float32)[:NL]
    rpb_ext = np.concatenate([rpb, np.full((NL, 8, 1, 31), NEG, np.float32)], axis=2)
    dr_m = np.where(valid, dr_idx, 15)
    g = rpb_ext[:, :, dr_m, dc_idx]
    bias_g = np.ascontiguousarray(np.transpose(g, (0, 2, 3, 4, 1, 5))).reshape(NL, 5, 128, 5, 8 * 128)
    common = {
        "bias_g": bias_g,
        "pool_w": np.ascontiguousarray(params["pool_w"][:NL], np.float32),
        "pool_scale": np.ascontiguousarray(params["pool_scale"][:NL], np.float32),
        "conv_w": np.ascontiguousarray(params["conv_w"][:NL], np.float32),
        "ln_g": np.ascontiguousarray(params["ln_g"][:NL], np.float32),
        "ln_b": np.ascontiguousarray(params["ln_b"][:NL], np.float32),
    }
    for k in ("ffn1_w_gate", "ffn1_w_up", "ffn1_w_down", "ffn2_w_gate", "ffn2_w_up", "ffn2_w_down", "w_in", "w_out"):
        common[k] = np.ascontiguousarray(params[k][:NL], np.float32)
    in_maps = []
    for xw, off in zip(xs, offs):
        m = dict(common)
        m["x"] = np.ascontiguousarray(xw, np.float32)
        m["invcnt"] = _invcnt(NTOK, off, S)
        in_maps.append(m)
    res = run_bass_kernel_spmd(nc, in_maps, core_ids=list(range(len(xs))))
    return [np.asarray(r["out"]) for r in res.results]


def kernel(**inputs):
    x = np.asarray(inputs["x"], np.float32)
    B, S, _ = x.shape
    NB = 10
    NTOK = NB * 512
    half = S // 2
    xs, offs = [], []
    for bi in range(B):
        xs.append(x[bi, 0:NTOK]); offs.append(0)
        xs.append(x[bi, S - NTOK:S]); offs.append(S - NTOK)
    outs = run_cores(xs, offs, S, inputs, 4, NB)
    y = np.empty((B, S, D), np.float32)
    for bi in range(B):
        y[bi, 0:half] = outs[2 * bi][0:half]
        y[bi, half:S] = outs[2 * bi + 1][NTOK - half:NTOK]
    return y
```

```python
import contextlib
import numpy as np
import concourse.bass as bass
import concourse.mybir as mybir
from concourse.bass_utils import run_bass_kernel_spmd

F32 = mybir.dt.float32
BF16 = mybir.dt.bfloat16
AF = mybir.ActivationFunctionType
ALU = mybir.AluOpType

D = 1024
DFF = 2816
NFC = DFF // 128
DIN = 2560
GRID_W = 64
ALPHA = 8.0 ** 0.25
LN_EPS = 1e-5
NEG = -30000.0

ENGS = ("pe", "act", "dve", "pool", "sp")
SEM_CAP = 30000
NDMA = 24


class Res:
    __slots__ = ("w", "rc", "rd")

    def __init__(self):
        self.w = None
        self.rc = {}
        self.rd = []


class Op:
    __slots__ = ("eng", "fn", "deps", "dma", "sem", "val", "marked", "prev")

    def __init__(self, eng, fn, dma):
        self.eng = eng
        self.fn = fn
        self.deps = []
        self.dma = dma
        self.sem = None
        self.val = 0
        self.marked = dma
        self.prev = None


class Prog:
    def __init__(self, nc):
        self.nc = nc
        self.ops = {e: [] for e in ENGS}

    def op(self, eng, fn, reads=(), writes=(), dma=False):
        o = Op(eng, fn, dma)
        deps = {}
        for r in reads:
            if r.w is not None:
                deps[id(r.w)] = r.w
        for r in writes:
            if r.w is not None:
                deps[id(r.w)] = r.w
            for x in r.rc.values():
                deps[id(x)] = x
            for x in r.rd:
                deps[id(x)] = x
        for d in deps.values():
            if d is o:
                continue
            if d.eng == "pe" and eng == "pe" and not d.dma and not dma:
                continue
            d.marked = True
            o.deps.append(d)
        for r in reads:
            if dma:
                r.rd.append(o)
            else:
                r.rc[eng] = o
        for r in writes:
            r.w = o
            r.rc = {}
            r.rd = []
        self.ops[eng].append(o)
        return o

    def dma(self, eng, out, in_, reads, writes, slow=False):
        if slow:
            return self.op(eng, lambda e: e.dma_start(out=out, in_=in_, allow_slow_non_contiguous=True), reads, writes, dma=True)
        return self.op(eng, lambda e: e.dma_start(out=out, in_=in_), reads, writes, dma=True)

    def emit(self, final_waits=()):
        nc = self.nc
        with contextlib.ExitStack() as st:
            def new_sem(name):
                return st.enter_context(nc.semaphore(name))
            for e in ENGS:
                cnt, cur, k = 0, None, 0
                pool, i = [], 0
                for o in self.ops[e]:
                    if o.dma:
                        if len(pool) < NDMA:
                            pool.append([new_sem(f"d_{e}_{len(pool)}"), 0])
                        slot = pool[i % NDMA]
                        i += 1
                        o.prev = (slot[0], slot[1])
                        slot[1] += 16
                        o.sem, o.val = slot[0], slot[1]
                    elif o.marked:
                        if cur is None or cnt >= SEM_CAP:
                            cur = new_sem(f"s_{e}_{k}")
                            k += 1
                            cnt = 0
                        cnt += 1
                        o.sem, o.val = cur, cnt
            block = st.enter_context(nc.Block())

            def run(e, eng):
                waited = {}
                for o in self.ops[e]:
                    need = {}
                    for d in o.deps:
                        key = id(d.sem)
                        if waited.get(key, 0) >= d.val:
                            continue
                        if key not in need or need[key][1] < d.val:
                            need[key] = (d.sem, d.val)
                    if o.dma and o.prev[1] > 0:
                        key = id(o.prev[0])
                        if waited.get(key, 0) < o.prev[1] and (key not in need or need[key][1] < o.prev[1]):
                            need[key] = o.prev
                    for key, (s, v) in need.items():
                        eng.wait_ge(s, v)
                        waited[key] = v
                    ins = o.fn(eng)
                    if o.marked:
                        ins.then_inc(o.sem, 16 if o.dma else 1)
                if e == "sp":
                    for d in final_waits:
                        eng.wait_ge(d.sem, d.val)

            block.tensor(lambda eng: run("pe", eng))
            block.scalar(lambda eng: run("act", eng))
            block.vector(lambda eng: run("dve", eng))
            block.gpsimd(lambda eng: run("pool", eng))
            block.sync(lambda eng: run("sp", eng))


def build_program(NL, NB):
    NT = 4 * NB
    NTOK = NT * 128
    KS_MAX = NT - 5
    W0_MAX = NT - 8
    nc = bass.Bass("TRN2", target_bir_lowering=False)

    def din(name, shape, dt=F32):
        return nc.dram_tensor(name, list(shape), dt, kind="ExternalInput").ap()

    def dscr(name, shape, dt):
        return nc.dram_tensor(name, list(shape), dt, kind="Internal").ap()

    x_in = din("x", [NTOK, D])
    w_f32 = {
        "g1": din("ffn1_w_gate", [NL, D, DFF]), "u1": din("ffn1_w_up", [NL, D, DFF]), "d1": din("ffn1_w_down", [NL, DFF, D]),
        "g2": din("ffn2_w_gate", [NL, D, DFF]), "u2": din("ffn2_w_up", [NL, D, DFF]), "d2": din("ffn2_w_down", [NL, DFF, D]),
        "win": din("w_in", [NL, D, DIN]), "wout": din("w_out", [NL, D, D]),
    }
    pool_w_in = din("pool_w", [NL, 4, 64, 64])
    pool_scale_in = din("pool_scale", [NL, 256])
    conv_w_in = din("conv_w", [NL, 3, 256])
    ln_g_in = din("ln_g", [NL, 3, D])
    ln_b_in = din("ln_b", [NL, 3, D])
    biasg_in = din("bias_g", [NL, 5, 128, 5, 8 * 128])
    invcnt_in = din("invcnt", [2, 128, NTOK])
    out = nc.dram_tensor("out", [NTOK, D], F32, kind="ExternalOutput").ap()

    w_bf = {k: dscr("bf_" + k, v.shape, BF16) for k, v in w_f32.items()}
    X1s = dscr("x1s", [NTOK, D], F32)
    UTs = [dscr(f"uts{i}", [256, NTOK + 16], F32) for i in range(2)]
    ZTs = [dscr(f"zts{i}", [256, NTOK + 2], F32) for i in range(2)]
    GBTs = dscr("gbts", [256, NTOK], F32)
    QTs = dscr("qts", [512, NTOK], BF16)
    KTs = [dscr(f"kts{i}", [512, NTOK], BF16) for i in range(2)]
    VAs = [dscr(f"vas{i}", [NTOK, 8 * 65], BF16) for i in range(2)]

    P = Prog(nc)
    with contextlib.ExitStack() as st:
        def T(name, shape, dt):
            return st.enter_context(nc.sbuf_tensor(name, list(shape), dt))

        X = T("X", [128, 4, D], F32)
        XM = T("XM", [128, 2, D], F32)
        XT = T("XT", [128, 8, 512], BF16)
        YT = T("YT", [128, 8, 512], BF16)
        GT = T("GT", [128, NFC, 512], BF16)
        SIL = T("SIL", [128, 2, 512], F32)
        NSLOT = 6
        RING = T("RING", [128, NSLOT, 8 * 512], BF16)
        LNP = T("LNP", [128, 2, 2, D], F32)
        KTW = T("KTW", [128, 4, 1024], BF16)
        VAW = T("VAW", [128, 8, 8 * 65], BF16)
        QM = T("QM", [128, 2, 4, 512], BF16)
        BT = T("BT", [128, 5, 8 * 128], BF16)
        PT = T("PT", [128, 2, 10 * 128], BF16)
        YC = T("YC", [128, 2, 512], BF16)
        IDF = T("IDF", [128, 128], F32)
        IDB = T("IDB", [128, 128], BF16)
        SH = T("SH", [128, 4, 528], F32)
        UB = SH[:, 0, :]
        SCR = T("SCR", [128, 2, 528], F32)
        SA = SCR[:, 0, :]
        SB = SCR[:, 1, :]
        ICN = SH[:, 2, 0:512]
        PTP = T("PTP", [128, 2, 512], BF16)
        ZB = SH[:, 1, 0:514]
        GBB = SH[:, 3, 0:512]
        CY = T("CY", [128, 2, 512], F32)
        PW = T("PW", [128, 2, 2, 128], BF16)
        PSC = T("PSC", [128, NL, 2], F32)
        CW = T("CW", [128, NL, 3, 2], F32)
        STF = T("STF", [128, 2, 512], F32)
        STB = T("STB", [128, 4, 512], BF16)
        VST = T("VST", [128, 3, 8 * 65], BF16)
        GCB = SCR[:, :, 0:512]
        ST6 = T("ST6", [128, 4, 2, 6], F32)
        MV = T("MV", [128, 4, 4], F32)
        RCP = T("RCP", [128, 2, 8], F32)
        ZERO = T("ZERO", [128, 16], F32)
        PSUM = st.enter_context(nc.psum_tensor("PSUM", [128, 8, 512], F32))

        rX = [[Res() for _ in range(2)] for _ in range(4)]
        rXM = [Res() for _ in range(2)]
        rXT = [Res() for _ in range(4)]
        rYT = [[Res() for _ in range(4)] for _ in range(8)]
        rGT = [Res() for _ in range(NFC)]
        rSIL = [Res() for _ in range(2)]
        rRING = [Res() for _ in range(NSLOT)]
        rLNP = [[Res(), Res()] for _ in range(2)]
        rKTW, rVAW, rBT = Res(), Res(), Res()
        rQM = [Res(), Res()]
        rPT = [Res() for _ in range(2)]
        rYC = [Res() for _ in range(2)]
        rID = Res()
        rSH = [Res() for _ in range(4)]
        rUB, rZB, rICN, rGBB = rSH
        rSA, rSB = Res(), Res()
        rCY = [Res(), Res()]
        rPTP = [Res() for _ in range(2)]
        rPW = [Res() for _ in range(2)]
        rSMALL = Res()
        rSTF = [Res() for _ in range(2)]
        rSTB = [Res() for _ in range(4)]
        rVST = [Res() for _ in range(3)]
        rGCB = [rSA, rSB]
        rST = [Res() for _ in range(4)]
        rST6 = [[Res(), Res()] for _ in range(4)]
        rRCP = [Res() for _ in range(2)]
        rZERO = Res()
        rBANK = [Res() for _ in range(8)]
        rX1s = [Res() for _ in range(NB)]
        rUTs = [[[Res() for _ in range(2)] for _ in range(NB)] for _ in range(2)]
        rZTs = [[[Res() for _ in range(2)] for _ in range(NB)] for _ in range(2)]
        rGBTs = [[Res() for _ in range(2)] for _ in range(NB)]
        rQTs = [[Res() for _ in range(4)] for _ in range(NB)]
        rKTs = [[[Res() for _ in range(4)] for _ in range(NB)] for _ in range(2)]
        rVAs = [[[Res() for _ in range(4)] for _ in range(NB)] for _ in range(2)]
        rPAD = Res()
        rWbf = {}

        state = {"bank": 0, "stf": 0, "stb": 0, "vst": 0, "sil": 0, "lnp": 0, "slot": 0, "xm": 0, "bst": 0,
                 "pt": 0, "yc": 0, "st": 0, "rcp": 0, "gcb": 0, "ptp": 0}

        def rot(key, n):
            i = state[key]
            state[key] = (i + 1) % n
            return i

        def bank():
            i = rot("bank", 6)
            return PSUM[:, i, :], rBANK[i]

        def mm(o, lhsT, rhs, start, stop, reads, writes, skip=False):
            if skip:
                P.op("pe", lambda e: e.matmul(o, lhsT, rhs, start=start, stop=stop, skip_group_check=True), reads, writes)
            else:
                P.op("pe", lambda e: e.matmul(o, lhsT, rhs, start=start, stop=stop), reads, writes)

        P.op("pool", lambda e: e.memset(IDF[:], 0.0), [], [rID])
        P.op("pool", lambda e: e.affine_select(out=IDF[:], in_=IDF[:], pattern=[[-1, 128]], compare_op=ALU.not_equal,
                                               fill=1.0, base=0, channel_multiplier=1), [rID], [rID])
        P.op("pool", lambda e: e.tensor_copy(out=IDB[:], in_=IDF[:]), [rID], [rID])
        P.op("pool", lambda e: e.memset(ZERO[:], 0.0), [], [rZERO])
        P.op("pool", lambda e: e.memset(VST[:], 1.0), [], rVST)
        P.op("pool", lambda e: e.memset(PW[:], 0.0), [], [rPW[0], rPW[1]])
        P.op("pool", lambda e: e.memset(QM[:], 0.0), [], rQM)
        for i in range(2):
            for c in range(2):
                rows = slice(c * 128, (c + 1) * 128)
                P.dma("sp", UTs[i][rows, 0:8], ZERO[:, 0:8], [rZERO], [rPAD])
                P.dma("sp", UTs[i][rows, NTOK + 8:NTOK + 16], ZERO[:, 0:8], [rZERO], [rPAD])
                P.dma("sp", ZTs[i][rows, 0:1], ZERO[:, 0:1], [rZERO], [rPAD], slow=True)
                P.dma("sp", ZTs[i][rows, NTOK + 1:NTOK + 2], ZERO[:, 0:1], [rZERO], [rPAD], slow=True)
        for l in range(NL):
            P.dma("sp", PSC[:, l, :], pool_scale_in[l, :].rearrange("(c p) -> p c", p=128), [], [rSMALL], slow=True)
            for j in range(3):
                P.dma("sp", CW[:, l, j, :], conv_w_in[l, j, :].rearrange("(c p) -> p c", p=128), [], [rSMALL], slow=True)

        def panel_list(l):
            out_ = []
            def ffn(g, u, d):
                for pn in range(6):
                    out_.append((g, "col", pn)); out_.append((u, "col", pn))
                for h in range(2):
                    for kp in range(3):
                        out_.append((d, "blk", (kp, h)))
            return out_, ffn

        def conv_src_dst(key, l, kind, idx):
            src, dst = w_f32[key], w_bf[key]
            if kind == "col":
                c0 = idx * 512
                w = min(512, src.shape[2] - c0)
                return src[l, :, c0:c0 + w], dst[l, :, c0:c0 + w]
            kp, h = idx
            r0 = kp * 1024
            nr = min(1024, src.shape[1] - r0)
            return src[l, r0:r0 + nr, h * 512:(h + 1) * 512], dst[l, r0:r0 + nr, h * 512:(h + 1) * 512]

        conv_queue = []

        def queue_layer_conversions(l):
            pieces = []
            for pn in range(6):
                pieces += [("g1", l, "col", pn), ("u1", l, "col", pn)]
            for h in range(2):
                for kp in range(3):
                    pieces.append(("d1", l, "blk", (kp, h)))
            for pn in range(5):
                pieces.append(("win", l, "col", pn))
            for h in range(2):
                pieces.append(("wout", l, "col", h))
            for pn in range(6):
                pieces += [("g2", l, "col", pn), ("u2", l, "col", pn)]
            for h in range(2):
                for kp in range(3):
                    pieces.append(("d2", l, "blk", (kp, h)))
            conv_queue.extend(pieces)

        def pump_conversions(n):
            for _ in range(min(n, len(conv_queue))):
                key, l, kind, idx = conv_queue.pop(0)
                s, d = conv_src_dst(key, l, kind, idx)
                r = Res()
                rWbf[(key, l, kind, idx)] = r
                P.dma("pool", d, s, [], [r])

        def panel_sequence():
            seq = []
            def ffn_p(l, g, u, d):
                for pn in range(6):
                    seq.append((g, l, "col", pn)); seq.append((u, l, "col", pn))
                for h in range(2):
                    for kp in range(3):
                        seq.append((d, l, "blk", (kp, h)))
            for it in range(NL + 1):
                for b_ in range(NB):
                    if it >= 1:
                        seq.append(("wout", it - 1, "col", 0)); seq.append(("wout", it - 1, "col", 1))
                        ffn_p(it - 1, "g2", "u2", "d2")
                    if it < NL:
                        ffn_p(it, "g1", "u1", "d1")
                        for pn in range(5):
                            seq.append(("win", it, "col", pn))
            return seq

        pseq = panel_sequence()
        pst = {"issued": 0, "acq": 0}
        released = [False] * len(pseq)
        pinfo = {}

        def issue_panel(n):
            key, l, kind, idx = pseq[n]
            while (key, l, kind, idx) not in rWbf:
                pump_conversions(1)
            s = n % NSLOT
            src = w_bf[key]
            if kind == "col":
                c0 = idx * 512
                w = min(512, src.shape[2] - c0)
                sap = src[l, :, c0:c0 + w].rearrange("(k p) f -> p k f", p=128)
                dap = RING[:, s, 0:8 * w].rearrange("p (k f) -> p k f", k=8)
                P.dma("sp", dap, sap, [rWbf[(key, l, kind, idx)]], [rRING[s]])
                pinfo[n] = (s, w)
            else:
                kp, h = idx
                r0 = kp * 1024
                nr = min(1024, src.shape[1] - r0)
                nk = nr // 128
                sap = src[l, r0:r0 + nr, h * 512:(h + 1) * 512].rearrange("(k p) f -> p k f", p=128)
                dap = RING[:, s, 0:nk * 512].rearrange("p (k f) -> p k f", k=nk)
                P.dma("sp", dap, sap, [rWbf[(key, l, kind, idx)]], [rRING[s]])
                pinfo[n] = (s, nk)

        def pump_panels():
            while pst["issued"] < len(pseq) and (pst["issued"] < NSLOT or released[pst["issued"] - NSLOT]):
                issue_panel(pst["issued"])
                pst["issued"] += 1

        def load_panel(key, l, kind, idx):
            n = pst["acq"]
            assert pseq[n] == (key, l, kind, idx), (pseq[n], key, l, kind, idx)
            pump_panels()
            assert n < pst["issued"]
            pst["acq"] += 1
            s, w = pinfo[n]
            return n, s, w

        def release(*ns):
            for n in ns:
                released[n] = True
            pump_panels()

        def pview(s, w):
            return RING[:, s, 0:8 * w].rearrange("p (k f) -> p k f", k=8)

        def load_lnp(l, idx):
            i = rot("lnp", 2)
            P.dma("sp", LNP[:, i, 0, :], ln_g_in[l, idx, :].partition_broadcast(128), [], [rLNP[i][0]])
            P.dma("sp", LNP[:, i, 1, :], ln_b_in[l, idx, :].partition_broadcast(128), [], [rLNP[i][1]])
            return i

        def stats_half(t, h):
            P.op("dve", lambda e: e.bn_stats(out=ST6[:, t, h, :], in_=X[:, t, h * 512:(h + 1) * 512]), [rX[t][h]], [rST6[t][h]])

        def ln_tile(t, lnp_i, eps):
            si = t
            P.op("dve", lambda e: e.bn_aggr(out=MV[:, si, 0:2], in_=ST6[:, si, :, :].rearrange("p a b -> p (a b)")),
                 rST6[si], [rST[si]])
            P.op("act", lambda e: e.activation(out=MV[:, si, 2:3], in_=MV[:, si, 1:2], func=AF.Sqrt, bias=float(eps), scale=1.0),
                 [rST[si]], [rST[si]])
            P.op("dve", lambda e: e.reciprocal(out=MV[:, si, 2:3], in_=MV[:, si, 2:3]), [rST[si]], [rST[si]])
            for h in range(2):
                hs = slice(h * 512, (h + 1) * 512)
                P.op("dve", lambda e, hs=hs: e.scalar_tensor_tensor(out=X[:, t, hs], in0=X[:, t, hs], scalar=MV[:, si, 0:1],
                                                                    in1=LNP[:, lnp_i, 0, hs], op0=ALU.subtract, op1=ALU.mult),
                     [rST[si], rX[t][h], rLNP[lnp_i][0]], [rX[t][h]])
                P.op("dve", lambda e, hs=hs: e.scalar_tensor_tensor(out=X[:, t, hs], in0=X[:, t, hs], scalar=MV[:, si, 2:3],
                                                                    in1=LNP[:, lnp_i, 1, hs], op0=ALU.mult, op1=ALU.add),
                     [rST[si], rX[t][h], rLNP[lnp_i][1]], [rX[t][h]])

        def transpose_tile(t):
            for g in range(2):
                xr = [rX[t][g]]
                bk, rb = bank()
                for k in range(4):
                    kk = g * 4 + k
                    P.op("pe", lambda e, kk=kk, k=k, bk=bk: e.transpose(bk[:, k * 128:(k + 1) * 128], X[:, t, kk * 128:(kk + 1) * 128], IDF[:]),
                         xr + [rID], [rb])
                dst = XT[:, g * 4:(g + 1) * 4, t * 128:(t + 1) * 128]
                srcv = bk.rearrange("p (a b) -> p a b", a=4)
                if g == 0:
                    P.op("dve", lambda e, dst=dst, srcv=srcv: e.tensor_copy(out=dst, in_=srcv), [rb], [rXT[t]])
                else:
                    P.op("act", lambda e, dst=dst, srcv=srcv: e.activation(out=dst, in_=srcv, func=AF.Copy), [rb], [rXT[t]])

        def ffn(l, gk, uk, dk, ln_idx, last):
            lnp_i = load_lnp(l, ln_idx)
            pending = None
            for pn in range(6):
                ng, sg, w = load_panel(gk, l, "col", pn)
                nu, su, _ = load_panel(uk, l, "col", pn)
                pg, pu = pview(sg, w), pview(su, w)
                nch = w // 128
                nsplit = 2 if pn == 0 else 0
                held = {}
                for c in range(nsplit):
                    held[c] = (bank(), bank())
                for half in range(2 if nsplit else 0):
                    cols = slice(half * 256, (half + 1) * 256)
                    rxt = [rXT[2 * half], rXT[2 * half + 1]]
                    for c in range(nsplit):
                        (bg, rbg), (bu, rbu) = held[c]
                        for k in range(8):
                            mm(bg[:, cols], pg[:, k, c * 128:(c + 1) * 128], XT[:, k, cols], k == 0, k == 7, rxt + [rRING[sg]], [rbg])
                        for k in range(8):
                            mm(bu[:, cols], pu[:, k, c * 128:(c + 1) * 128], XT[:, k, cols], k == 0, k == 7, rxt + [rRING[su]], [rbu])
                for c in range(nch):
                    fc = pn * 4 + c
                    if c < nsplit:
                        (bg, rbg), (bu, rbu) = held[c]
                    else:
                        bg, rbg = bank()
                        bu, rbu = bank()
                        for k in range(8):
                            mm(bg, pg[:, k, c * 128:(c + 1) * 128], XT[:, k, :], k == 0, k == 7, rXT + [rRING[sg]], [rbg])
                        for k in range(8):
                            mm(bu, pu[:, k, c * 128:(c + 1) * 128], XT[:, k, :], k == 0, k == 7, rXT + [rRING[su]], [rbu])
                    si = rot("sil", 2)
                    P.op("act", lambda e, si=si, bg=bg: e.activation(out=SIL[:, si, :], in_=bg, func=AF.Silu), [rbg], [rSIL[si]])
                    P.op("dve", lambda e, si=si, bu=bu, fc=fc: e.tensor_tensor(out=GT[:, fc, :], in0=SIL[:, si, :], in1=bu, op=ALU.mult),
                         [rSIL[si], rbu], [rGT[fc]])
                release(ng, nu)
            for h in range(2):
                slots = [load_panel(dk, l, "blk", (kp, h)) for kp in range(3)]
                for t in range(4):
                    by, rby = bank()
                    for fc in range(NFC):
                        _, s, nk = slots[fc // 8]
                        pv = RING[:, s, 0:nk * 512].rearrange("p (k f) -> p k f", k=nk)
                        mm(by, GT[:, fc, t * 128:(t + 1) * 128], pv[:, fc % 8, :], fc == 0, fc == NFC - 1,
                           [rGT[fc], rRING[s]], [rby])
                    P.op("dve", lambda e, t=t, h=h, by=by: e.scalar_tensor_tensor(
                        out=X[:, t, h * 512:(h + 1) * 512], in0=X[:, t, h * 512:(h + 1) * 512], scalar=2.0 * ALPHA,
                        in1=by, op0=ALU.mult, op1=ALU.add), [rX[t][h], rby], [rX[t][h]])
                    stats_half(t, h)
                    if h == 1:
                        ln_tile(t, lnp_i, 4.0 * LN_EPS)
                        if not last and t >= 1:
                            transpose_tile(t - 1)
                release(*[x[0] for x in slots])
            if not last:
                transpose_tile(3)

        def alt_view(k):
            return GT[:, k * 10:(k + 1) * 10, :].rearrange("p a b -> p (a b)").rearrange("p (i f) -> p i f", i=5)

        ALT = [alt_view(0), alt_view(1)]
        rALT = [rGT[0:10], rGT[10:20]]

        def prefetch_tables(l, b):
            if b == 0:
                P.dma("pool", BT[:], biasg_in[l, 0], [], [rBT])
                P.dma("pool", ALT[0], biasg_in[l, 1], [], rALT[0])
                P.dma("pool", ALT[1], biasg_in[l, 2], [], rALT[1])
            if b == NB - 1:
                P.dma("pool", ALT[0], biasg_in[l, 3], [], rALT[0])
                P.dma("pool", ALT[1], biasg_in[l, 4], [], rALT[1])


        def mixer_window(b):
            w0 = min(max(4 * b - 2, 0), W0_MAX)
            wblocks = sorted(set(min(max((w0 + i) // 4, 0), NB - 1) for i in range(8)))
            return w0, wblocks

        def mixer_chunk_loads(l, b, c):
            par = l % 2
            b0 = b * 512
            nbr = [bb for bb in (b - 1, b, b + 1) if 0 <= bb < NB]
            rows = slice(c * 128, (c + 1) * 128)
            P.dma("sp", UB[:], UTs[par][rows, b0:b0 + 528], [rUTs[par][bb][c] for bb in nbr] + [rPAD], [rUB])
            P.dma("sp", ICN[:], invcnt_in[c, :, b0:b0 + 512], [], [rICN])
            P.dma("sp", ZB[:], ZTs[par][rows, b0:b0 + 514], [rZTs[par][bb][c] for bb in nbr] + [rPAD], [rZB])
            P.dma("sp", GBB[:], GBTs[rows, b0:b0 + 512], [rGBTs[b][c]], [rGBB])

        def mixer_prefetch(l, b):
            par = l % 2
            b0 = b * 512
            w0, wblocks = mixer_window(b)
            mixer_chunk_loads(l, b, 0)
            qsrc = QTs[:, b0:b0 + 512].rearrange("(c p) t -> p c t", p=128)
            P.dma("sp", QM[0:64, 0, :, :], qsrc[0:64], rQTs[b], [rQM[0]])
            P.dma("sp", QM[64:128, 1, :, :], qsrc[64:128], rQTs[b], [rQM[1]])
            P.dma("sp", KTW[:], KTs[par][:, w0 * 128:w0 * 128 + 1024].rearrange("(c p) t -> p c t", p=128),
                  [r for bb in wblocks for r in rKTs[par][bb]], [rKTW])
            P.dma("sp", VAW[:], VAs[par][w0 * 128:w0 * 128 + 1024, :].rearrange("(t p) f -> p t f", p=128),
                  [r for bb in wblocks for r in rVAs[par][bb]], [rVAW])

        def mixer(l, b, prefetched):
            par = l % 2
            b0 = b * 512
            w0, wblocks = mixer_window(b)
            nbr = [bb for bb in (b - 1, b, b + 1) if 0 <= bb < NB]
            if b == 0:
                for g in range(4):
                    c, hh = g // 2, g % 2
                    P.dma("pool", PW[hh * 64:(hh + 1) * 64, par, c, hh * 64:(hh + 1) * 64], pool_w_in[l, g, :, :], [], [rPW[par]])
            if not prefetched:
                mixer_prefetch(l, b)
            lnp_i = load_lnp(l, 1)
            lo, hi = slice(0, 64), slice(64, 128)

            def add(o, a_, b_, rr, ww):
                P.op("pool", lambda e: e.tensor_tensor(out=o, in0=a_, in1=b_, op=ALU.add), rr, ww)

            def elementwise(c):
                add(SA[:, 1:528], UB[:, 1:528], UB[:, 0:527], [rUB], [rSA])
                if c == 0:
                    add(SB[hi, 3:528], SA[hi, 3:528], SA[hi, 1:526], [rSA], [rSB])
                    sel_lo, sel_hi = SA[lo, 8:520], SB[hi, 9:521]
                else:
                    add(SB[:, 3:528], SA[:, 3:528], SA[:, 1:526], [rSA], [rSB])
                    add(SA[:, 7:528], SB[:, 7:528], SB[:, 3:524], [rSB], [rSA])
                    add(SB[hi, 15:528], SA[hi, 15:528], SA[hi, 7:520], [rSA], [rSB])
                    sel_lo, sel_hi = SA[lo, 11:523], SB[hi, 15:527]
                for part, sel in ((lo, sel_lo), (hi, sel_hi)):
                    P.op("pool", lambda e, part=part, sel=sel: e.tensor_tensor(out=CY[part, 0, :], in0=sel, in1=ICN[part, :], op=ALU.mult),
                         [rSA, rSB, rICN], [rCY[0]])
                    P.op("pool", lambda e, part=part, c=c: e.tensor_tensor(out=PTP[part, c, :], in0=CY[part, 0, :], in1=UB[part, 8:520], op=ALU.subtract),
                         [rCY[0], rUB], [rPTP[c]])
                P.op("dve", lambda e, c=c: e.tensor_scalar(out=CY[:, 1, :], in0=ZB[:, 1:513], scalar1=CW[:, l, 1, c:c + 1], scalar2=None, op0=ALU.mult),
                     [rZB, rSMALL], [rCY[1]])
                P.op("dve", lambda e, c=c: e.scalar_tensor_tensor(out=CY[:, 1, :], in0=ZB[:, 0:512], scalar=CW[:, l, 0, c:c + 1], in1=CY[:, 1, :],
                                                                  op0=ALU.mult, op1=ALU.add), [rZB, rSMALL, rCY[1]], [rCY[1]])
                P.op("dve", lambda e, c=c: e.scalar_tensor_tensor(out=CY[:, 1, :], in0=ZB[:, 2:514], scalar=CW[:, l, 2, c:c + 1], in1=CY[:, 1, :],
                                                                  op0=ALU.mult, op1=ALU.add), [rZB, rSMALL, rCY[1]], [rCY[1]])
                P.op("dve", lambda e, c=c: e.tensor_tensor(out=YT[:, 2 + c, :], in0=CY[:, 1, :], in1=GBB[:], op=ALU.mult),
                     [rCY[1], rGBB], rYT[2 + c])

            elementwise(0)
            mixer_chunk_loads(l, b, 1)

            ob = [(PSUM[:, 6, :], rBANK[6]), (PSUM[:, 7, :], rBANK[7])]
            tile_ctx = {}

            def scores(t, hp):
                j = 4 * b + t
                if hp == 0:
                    ks = min(max(j - 2, 0), KS_MAX)
                    tile_ctx[t] = (ks - w0, rot("yc", 2), rot("rcp", 2))
                kw, yi, ri = tile_ctx[t]
                tid = {0: 1, 1: 2, NT - 2: 3, NT - 1: 4}.get(j, 0)
                if tid == 0:
                    BTv, rBTv = BT, [rBT]
                else:
                    BTv, rBTv = ALT[(tid - 1) % 2], rALT[(tid - 1) % 2]
                pi = rot("pt", 2)
                bA, rA = bank()
                bB, rB = bank()
                bC, rC = bank()
                hsl = slice(2 * hp * 128, (2 * hp + 2) * 128)
                mm(bA.rearrange("p (a b) -> p a b", a=2), IDB[:], BTv[:, 0:2, hsl], True, False, [rID] + rBTv, [rA], skip=True)
                mm(bB.rearrange("p (a b) -> p a b", a=2), IDB[:], BTv[:, 2:4, hsl], True, False, [rID] + rBTv, [rB], skip=True)
                mm(bC[:, 0:256], IDB[:], BTv[:, 4, hsl], True, False, [rID] + rBTv, [rC], skip=True)
                for i in range(5):
                    bk_, rr = ((bA, rA), (bB, rB), (bC, rC))[i // 2]
                    o = bk_[:, (i % 2) * 256:(i % 2 + 1) * 256].rearrange("p (a b) -> p a b", a=2)
                    mm(o, KTW[:, hp, (kw + i) * 128:(kw + i + 1) * 128], QM[:, :, hp, t * 128:(t + 1) * 128],
                       False, True, [rKTW] + rQM, [rr], skip=True)
                P.op("act", lambda e: e.activation(out=PT[:, pi, 0:512], in_=bA, func=AF.Exp), [rA], [rPT[pi]])
                P.op("act", lambda e: e.activation(out=PT[:, pi, 512:1024], in_=bB, func=AF.Exp), [rB], [rPT[pi]])
                P.op("act", lambda e: e.activation(out=PT[:, pi, 1024:1280], in_=bC[:, 0:256], func=AF.Exp), [rC], [rPT[pi]])

                def pv():
                    for hh in range(2):
                        h = 2 * hp + hh
                        obk, obr = ob[h // 4]
                        o = obk[:, (h % 4) * 65:(h % 4 + 1) * 65]
                        for i in range(5):
                            c0 = i * 256 + hh * 128
                            mm(o, PT[:, pi, c0:c0 + 128], VAW[:, kw + i, h * 65:(h + 1) * 65], i == 0, i == 4,
                               [rPT[pi], rVAW], [obr])
                    if hp % 2 == 1:
                        g = hp // 2
                        obk, obr = ob[g]
                        ov = obk[:, 0:260].rearrange("p (h d) -> p h d", h=4)
                        P.op("dve", lambda e: e.reciprocal(out=RCP[:, ri, g * 4:(g + 1) * 4], in_=ov[:, :, 64]), [obr], [rRCP[ri]])
                        for h4 in range(4):
                            h = g * 4 + h4
                            P.op("dve", lambda e, h4=h4, h=h: e.tensor_scalar(
                                out=YC[:, yi, h * 64:(h + 1) * 64], in0=ov[:, h4, 0:64], scalar1=RCP[:, ri, h:h + 1], scalar2=None, op0=ALU.mult),
                                [obr, rRCP[ri]], [rYC[yi]])
                    if hp != 3:
                        return None

                    def tr():
                        bk, rb = bank()
                        bkb = bk.bitcast(BF16)
                        for c in range(4):
                            P.op("pe", lambda e, c=c: e.transpose(bkb[:, c * 128:(c + 1) * 128], YC[:, yi, c * 128:(c + 1) * 128], IDB[:]),
                                 [rYC[yi], rID], [rb])
                        P.op("act", lambda e: e.activation(out=YT[:, 4:8, t * 128:(t + 1) * 128],
                                                           in_=bkb[:, 0:512].rearrange("p (a b) -> p a b", a=4), func=AF.Copy),
                             [rb], [rYT[4 + c][t] for c in range(4)])
                    return tr
                return pv

            queue = []
            n = 0
            for t in range(4):
                for hp in range(4):
                    pvf = scores(t, hp)
                    due = [f for (d, f) in queue if d <= n]
                    queue = [(d, f) for (d, f) in queue if d > n]
                    for f in due:
                        r_ = f()
                        if r_ is not None:
                            queue.append((n + 1, r_))
                    queue.append((n + 1, pvf))
                    n += 1
                if t == 1:
                    elementwise(1)
            while queue:
                d, f = queue.pop(0)
                r_ = f()
                if r_ is not None:
                    queue.append((d + 1, r_))

            for c in range(2):
                bk, rb = bank()
                mm(bk, PW[:, par, c, :], PTP[:, c, :], True, True, [rPW[par], rPTP[c]], [rb])
                P.op("act", lambda e, c=c, bk=bk: e.activation(out=YT[:, c, :], in_=bk, func=AF.Copy, scale=PSC[:, l, c:c + 1]),
                     [rb, rSMALL], rYT[c])

            panels = [load_panel("wout", l, "col", h) for h in range(2)]
            for t in range(4):
                xi = rot("xm", 2)
                P.dma("sp", XM[:, xi, :], X1s[b0 + t * 128:b0 + (t + 1) * 128, :], [rX1s[b]], [rXM[xi]])
                for h in range(2):
                    _, s_, w = panels[h]
                    pv_ = pview(s_, w)
                    by, rby = bank()
                    for k in range(8):
                        mm(by, YT[:, k, t * 128:(t + 1) * 128], pv_[:, k, :], k == 0, k == 7,
                           [rYT[k][tt] for tt in range(4)] + [rRING[s_]], [rby])
                    P.op("dve", lambda e, t=t, h=h, by=by, xi=xi: e.scalar_tensor_tensor(
                        out=X[:, t, h * 512:(h + 1) * 512], in0=XM[:, xi, h * 512:(h + 1) * 512], scalar=ALPHA,
                        in1=by, op0=ALU.mult, op1=ALU.add), [rXM[xi], rby], [rX[t][h]])
                    stats_half(t, h)
                ln_tile(t, lnp_i, LN_EPS)
                if t >= 1:
                    transpose_tile(t - 1)
            release(panels[0][0], panels[1][0])
            transpose_tile(3)

        SQ = "act"

        def project(l, b):
            par = l % 2
            b0 = b * 512
            P.dma(SQ, X1s[b0:b0 + 512, :].rearrange("(t p) d -> p t d", p=128), X[:],
                  [rX[t][h] for t in range(4) for h in range(2)], [rX1s[b]])
            for pn in range(4):
                n_, s, w = load_panel("win", l, "col", pn)
                pv = pview(s, w)
                for c in range(4):
                    bk, rb = bank()
                    for k in range(8):
                        mm(bk, pv[:, k, c * 128:(c + 1) * 128], XT[:, k, :], k == 0, k == 7, rXT + [rRING[s]], [rb])
                    if pn == 0:
                        i = rot("stf", 2)
                        if c < 2:
                            P.op("act", lambda e, i=i, bk=bk: e.activation(out=STF[:, i, :], in_=bk, func=AF.Copy), [rb], [rSTF[i]])
                            P.dma(SQ, UTs[par][c * 128:(c + 1) * 128, 8 + b0:8 + b0 + 512], STF[:, i, :], [rSTF[i]], [rUTs[par][b][c]])
                        else:
                            P.op("dve", lambda e, i=i, bk=bk: e.tensor_copy(out=STF[:, i, :], in_=bk), [rb], [rSTF[i]])
                            P.dma(SQ, GBTs[(c - 2) * 128:(c - 1) * 128, b0:b0 + 512], STF[:, i, :], [rSTF[i]], [rGBTs[b][c - 2]])
                    elif pn == 1:
                        if c < 2:
                            gi = c
                            P.op("act", lambda e, gi=gi, bk=bk: e.activation(out=GCB[:, gi, :], in_=bk, func=AF.Copy), [rb], [rGCB[gi]])
                        else:
                            gi = c - 2
                            i = rot("stf", 2)
                            P.op("dve", lambda e, i=i, gi=gi, bk=bk: e.tensor_tensor(out=STF[:, i, :], in0=GCB[:, gi, :], in1=bk, op=ALU.mult),
                                 [rb, rGCB[gi]], [rSTF[i]])
                            P.dma(SQ, ZTs[par][gi * 128:(gi + 1) * 128, 1 + b0:1 + b0 + 512], STF[:, i, :], [rSTF[i]], [rZTs[par][b][gi]])
                    elif pn == 2:
                        i = rot("stb", 4)
                        P.op("act", lambda e, i=i, bk=bk: e.activation(out=STB[:, i, :], in_=bk, func=AF.Copy, scale=0.125), [rb], [rSTB[i]])
                        P.dma(SQ, QTs[c * 128:(c + 1) * 128, b0:b0 + 512], STB[:, i, :], [rSTB[i]], [rQTs[b][c]])
                    else:
                        i = rot("stb", 4)
                        P.op("dve", lambda e, i=i, bk=bk: e.tensor_copy(out=STB[:, i, :], in_=bk), [rb], [rSTB[i]])
                        P.dma(SQ, KTs[par][c * 128:(c + 1) * 128, b0:b0 + 512], STB[:, i, :], [rSTB[i]], [rKTs[par][b][c]])
                release(n_)
            n_, s, w = load_panel("win", l, "col", 4)
            pv = pview(s, w)
            for t in range(4):
                bk, rb = bank()
                for k in range(8):
                    mm(bk, XT[:, k, t * 128:(t + 1) * 128], pv[:, k, :], k == 0, k == 7, [rXT[t], rRING[s]], [rb])
                vi = rot("vst", 3)
                dst = VST[:, vi, :].rearrange("p (h d) -> p h d", h=8)[:, :, 0:64]
                srcv = bk.rearrange("p (h d) -> p h d", h=8)
                if t % 2 == 0:
                    P.op("act", lambda e, dst=dst, srcv=srcv: e.activation(out=dst, in_=srcv, func=AF.Copy), [rb], [rVST[vi]])
                else:
                    P.op("dve", lambda e, dst=dst, srcv=srcv: e.tensor_copy(out=dst, in_=srcv), [rb], [rVST[vi]])
                P.dma(SQ, VAs[par][b0 + t * 128:b0 + (t + 1) * 128, :], VST[:, vi, :], [rVST[vi]], [rVAs[par][b][t]])
            release(n_)

        queue_layer_conversions(0)
        pump_conversions(len(conv_queue))
        out_stores = []
        for it in range(NL + 1):
            if it + 1 < NL:
                queue_layer_conversions(it + 1)
            per_block = (len(conv_queue) + NB - 1) // NB if conv_queue else 0
            for b in range(NB):
                b0 = b * 512
                if it >= 1:
                    mixer(it - 1, b, prefetched=(b >= 1))
                    ffn(it - 1, "g2", "u2", "d2", 2, last=(it == NL))
                else:
                    P.dma("sp", X[:], x_in[b0:b0 + 512, :].rearrange("(t p) d -> p t d", p=128), [],
                          [rX[t][h] for t in range(4) for h in range(2)])
                    for t in range(4):
                        transpose_tile(t)
                if it == NL:
                    out_stores.append(P.dma("sp", out[b0:b0 + 512, :].rearrange("(t p) d -> p t d", p=128), X[:],
                                            [rX[t][h] for t in range(4) for h in range(2)], []))
                    if b + 1 < NB:
                        mixer_prefetch(it - 1, b + 1)
                        prefetch_tables(it - 1, b + 1)
                else:
                    if it >= 1 and b + 1 < NB:
                        mixer_prefetch(it - 1, b + 1)
                    ffn(it, "g1", "u1", "d1", 0, last=False)
                    if b + 1 < NB:
                        if it >= 1:
                            prefetch_tables(it - 1, b + 1)
                    else:
                        prefetch_tables(it, 0)
                    project(it, b)
                pump_conversions(per_block)
        P.emit(final_waits=out_stores)
    return nc


def _tables(NT):
    R = 2 * NT
    KS_MAX = NT - 5
    tiles = [min(10, NT // 2), 0, 1, NT - 2, NT - 1]
    dr_idx = np.zeros((5, 128, 5, 128), np.int64)
    dc_idx = np.zeros((5, 128, 5, 128), np.int64)
    valid = np.zeros((5, 128, 5, 128), bool)
    kk = np.arange(128)[:, None]
    qq = np.arange(128)[None, :]
    for ti, j in enumerate(tiles):
        ks = min(max(j - 2, 0), KS_MAX)
        for i in range(5):
            kt = ks + i
            krow = 2 * kt + kk // 64
            kcol = kk % 64
            qrow = 2 * j + qq // 64
            qcol = qq % 64
            rs = np.clip(qrow - 4, 0, R - 8)
            cs = np.clip(qcol - 8, 0, GRID_W - 16)
            v = (krow >= rs) & (krow < rs + 8) & (kcol >= cs) & (kcol < cs + 16)
            dr = np.clip(krow - qrow + 7, 0, 14)
            dc = np.clip(kcol - qcol, -15, 15) + 15
            dr_idx[ti, :, i, :] = np.broadcast_to(dr, (128, 128))
            dc_idx[ti, :, i, :] = np.broadcast_to(dc, (128, 128))
            valid[ti, :, i, :] = v
    return dr_idx, dc_idx, valid


def _invcnt(NTOK, off, S):
    t = np.arange(NTOK) + off
    res = np.zeros((2, 128, NTOK), np.float32)
    for g, w in enumerate((2, 4, 8, 16)):
        lo = np.clip(t - w // 2, 0, S)
        hi = np.clip(t - w // 2 + w, 0, S)
        cnt = np.maximum(hi - lo, 1).astype(np.float32)
        c, hh = g // 2, g % 2
        res[c, hh * 64:(hh + 1) * 64, :] = (1.0 / cnt)[None, :]
    return res


_CACHE = {}


def run_cores(xs, offs, S, params, NL, NB):
    NT = 4 * NB
    NTOK = NT * 128
    key = (NL, NB)
    if key not in _CACHE:
        _CACHE[key] = build_program(NL, NB)
    nc = _CACHE[key]
    dr_idx, dc_idx, valid = _tables(NT)
    rpb = np.asarray(params["rpb"], np.float32)[:NL]
    rpb_ext = np.concatenate([rpb, np.full((NL, 8, 1, 31), NEG, np.float32)], axis=2)
    dr_m = np.where(valid, dr_idx, 15)
    g = rpb_ext[:, :, dr_m, dc_idx]
    bias_g = np.ascontiguousarray(np.transpose(g, (0, 2, 3, 4, 1, 5))).reshape(NL, 5, 128, 5, 8 * 128)
    common = {
        "bias_g": bias_g,
        "pool_w": np.ascontiguousarray(params["pool_w"][:NL], np.float32),
        "pool_scale": np.ascontiguousarray(params["pool_scale"][:NL], np.float32),
        "conv_w": np.ascontiguousarray(params["conv_w"][:NL], np.float32),
        "ln_g": np.ascontiguousarray(params["ln_g"][:NL], np.float32),
        "ln_b": np.ascontiguousarray(params["ln_b"][:NL], np.float32),
    }
    for k in ("ffn1_w_gate", "ffn1_w_up", "ffn1_w_down", "ffn2_w_gate", "ffn2_w_up", "ffn2_w_down", "w_in", "w_out"):
        common[k] = np.ascontiguousarray(params[k][:NL], np.float32)
    in_maps = []
    for xw, off in zip(xs, offs):
        m = dict(common)
        m["x"] = np.ascontiguousarray(xw, np.float32)
        m["invcnt"] = _invcnt(NTOK, off, S)
        in_maps.append(m)
    res = run_bass_kernel_spmd(nc, in_maps, core_ids=list(range(len(xs))))
    return [np.asarray(r["out"]) for r in res.results]


def kernel(**inputs):
    x = np.asarray(inputs["x"], np.float32)
    B, S, _ = x.shape
    NB = 10
    NTOK = NB * 512
    half = S // 2
    xs, offs = [], []
    for bi in range(B):
        xs.append(x[bi, 0:NTOK]); offs.append(0)
        xs.append(x[bi, S - NTOK:S]); offs.append(S - NTOK)
    outs = run_cores(xs, offs, S, inputs, 4, NB)
    y = np.empty((B, S, D), np.float32)
    for bi in range(B):
        y[bi, 0:half] = outs[2 * bi][0:half]
        y[bi, half:S] = outs[2 * bi + 1][NTOK - half:NTOK]
    return y
```

```python
import contextlib
import numpy as np
import concourse.bass as bass
import concourse.mybir as mybir
from concourse.bass_utils import run_bass_kernel_spmd

F32 = mybir.dt.float32
BF16 = mybir.dt.bfloat16
AF = mybir.ActivationFunctionType
ALU = mybir.AluOpType

D = 1024
DFF = 2816
NFC = DFF // 128
DIN = 2560
GRID_W = 64
ALPHA = 8.0 ** 0.25
LN_EPS = 1e-5
NEG = -30000.0

ENGS = ("pe", "act", "dve", "pool", "sp")
SEM_CAP = 30000
NDMA = 24


class Res:
    __slots__ = ("w", "rc", "rd")

    def __init__(self):
        self.w = None
        self.rc = {}
        self.rd = []


class Op:
    __slots__ = ("eng", "fn", "deps", "dma", "sem", "val", "marked", "prev")

    def __init__(self, eng, fn, dma):
        self.eng = eng
        self.fn = fn
        self.deps = []
        self.dma = dma
        self.sem = None
        self.val = 0
        self.marked = dma
        self.prev = None


class Prog:
    def __init__(self, nc):
        self.nc = nc
        self.ops = {e: [] for e in ENGS}

    def op(self, eng, fn, reads=(), writes=(), dma=False):
        o = Op(eng, fn, dma)
        deps = {}
        for r in reads:
            if r.w is not None:
                deps[id(r.w)] = r.w
        for r in writes:
            if r.w is not None:
                deps[id(r.w)] = r.w
            for x in r.rc.values():
                deps[id(x)] = x
            for x in r.rd:
                deps[id(x)] = x
        for d in deps.values():
            if d is o:
                continue
            if d.eng == "pe" and eng == "pe" and not d.dma and not dma:
                continue
            d.marked = True
            o.deps.append(d)
        for r in reads:
            if dma:
                r.rd.append(o)
            else:
                r.rc[eng] = o
        for r in writes:
            r.w = o
            r.rc = {}
            r.rd = []
        self.ops[eng].append(o)
        return o

    def dma(self, eng, out, in_, reads, writes, slow=False):
        if slow:
            return self.op(eng, lambda e: e.dma_start(out=out, in_=in_, allow_slow_non_contiguous=True), reads, writes, dma=True)
        return self.op(eng, lambda e: e.dma_start(out=out, in_=in_), reads, writes, dma=True)

    def emit(self, final_waits=()):
        nc = self.nc
        with contextlib.ExitStack() as st:
            def new_sem(name):
                return st.enter_context(nc.semaphore(name))
            for e in ENGS:
                cnt, cur, k = 0, None, 0
                pool, i = [], 0
                for o in self.ops[e]:
                    if o.dma:
                        if len(pool) < NDMA:
                            pool.append([new_sem(f"d_{e}_{len(pool)}"), 0])
                        slot = pool[i % NDMA]
                        i += 1
                        o.prev = (slot[0], slot[1])
                        slot[1] += 16
                        o.sem, o.val = slot[0], slot[1]
                    elif o.marked:
                        if cur is None or cnt >= SEM_CAP:
                            cur = new_sem(f"s_{e}_{k}")
                            k += 1
                            cnt = 0
                        cnt += 1
                        o.sem, o.val = cur, cnt
            block = st.enter_context(nc.Block())

            def run(e, eng):
                waited = {}
                for o in self.ops[e]:
                    need = {}
                    for d in o.deps:
                        key = id(d.sem)
                        if waited.get(key, 0) >= d.val:
                            continue
                        if key not in need or need[key][1] < d.val:
                            need[key] = (d.sem, d.val)
                    if o.dma and o.prev[1] > 0:
                        key = id(o.prev[0])
                        if waited.get(key, 0) < o.prev[1] and (key not in need or need[key][1] < o.prev[1]):
                            need[key] = o.prev
                    for key, (s, v) in need.items():
                        eng.wait_ge(s, v)
                        waited[key] = v
                    ins = o.fn(eng)
                    if o.marked:
                        ins.then_inc(o.sem, 16 if o.dma else 1)
                if e == "sp":
                    for d in final_waits:
                        eng.wait_ge(d.sem, d.val)

            block.tensor(lambda eng: run("pe", eng))
            block.scalar(lambda eng: run("act", eng))
            block.vector(lambda eng: run("dve", eng))
            block.gpsimd(lambda eng: run("pool", eng))
            block.sync(lambda eng: run("sp", eng))


def build_program(NL, NB):
    NT = 4 * NB
    NTOK = NT * 128
    KS_MAX = NT - 5
    W0_MAX = NT - 8
    nc = bass.Bass("TRN2", target_bir_lowering=False)

    def din(name, shape, dt=F32):
        return nc.dram_tensor(name, list(shape), dt, kind="ExternalInput").ap()

    def dscr(name, shape, dt):
        return nc.dram_tensor(name, list(shape), dt, kind="Internal").ap()

    x_in = din("x", [NTOK, D])
    w_f32 = {
        "g1": din("ffn1_w_gate", [NL, D, DFF]), "u1": din("ffn1_w_up", [NL, D, DFF]), "d1": din("ffn1_w_down", [NL, DFF, D]),
        "g2": din("ffn2_w_gate", [NL, D, DFF]), "u2": din("ffn2_w_up", [NL, D, DFF]), "d2": din("ffn2_w_down", [NL, DFF, D]),
        "win": din("w_in", [NL, D, DIN]), "wout": din("w_out", [NL, D, D]),
    }
    pool_w_in = din("pool_w", [NL, 4, 64, 64])
    pool_scale_in = din("pool_scale", [NL, 256])
    conv_w_in = din("conv_w", [NL, 3, 256])
    ln_g_in = din("ln_g", [NL, 3, D])
    ln_b_in = din("ln_b", [NL, 3, D])
    biasg_in = din("bias_g", [NL, 5, 128, 5, 8 * 128])
    invcnt_in = din("invcnt", [2, 128, NTOK])
    out = nc.dram_tensor("out", [NTOK, D], F32, kind="ExternalOutput").ap()

    w_bf = {k: dscr("bf_" + k, v.shape, BF16) for k, v in w_f32.items()}
    X1s = dscr("x1s", [NTOK, D], F32)
    UTs = [dscr(f"uts{i}", [256, NTOK + 16], F32) for i in range(2)]
    ZTs = [dscr(f"zts{i}", [256, NTOK + 2], F32) for i in range(2)]
    GBTs = dscr("gbts", [256, NTOK], F32)
    QTs = dscr("qts", [512, NTOK], BF16)
    KTs = [dscr(f"kts{i}", [512, NTOK], BF16) for i in range(2)]
    VAs = [dscr(f"vas{i}", [NTOK, 8 * 65], BF16) for i in range(2)]

    P = Prog(nc)
    with contextlib.ExitStack() as st:
        def T(name, shape, dt):
            return st.enter_context(nc.sbuf_tensor(name, list(shape), dt))

        X = T("X", [128, 4, D], F32)
        XM = T("XM", [128, 2, D], F32)
        XT = T("XT", [128, 8, 512], BF16)
        YT = T("YT", [128, 8, 512], BF16)
        GT = T("GT", [128, NFC, 512], BF16)
        SIL = T("SIL", [128, 2, 512], F32)
        NSLOT = 6
        RING = T("RING", [128, NSLOT, 8 * 512], BF16)
        LNP = T("LNP", [128, 2, 2, D], F32)
        KTW = T("KTW", [128, 4, 1024], BF16)
        VAW = T("VAW", [128, 8, 8 * 65], BF16)
        QM = T("QM", [128, 2, 4, 512], BF16)
        BT = T("BT", [128, 5, 8 * 128], BF16)
        PT = T("PT", [128, 2, 10 * 128], BF16)
        YC = T("YC", [128, 2, 512], BF16)
        IDF = T("IDF", [128, 128], F32)
        IDB = T("IDB", [128, 128], BF16)
        SH = T("SH", [128, 4, 528], F32)
        UB = SH[:, 0, :]
        SCR = T("SCR", [128, 2, 528], F32)
        SA = SCR[:, 0, :]
        SB = SCR[:, 1, :]
        ICN = SH[:, 2, 0:512]
        PTP = T("PTP", [128, 2, 512], BF16)
        ZB = SH[:, 1, 0:514]
        GBB = SH[:, 3, 0:512]
        CY = T("CY", [128, 2, 512], F32)
        PW = T("PW", [128, 2, 2, 128], BF16)
        PSC = T("PSC", [128, NL, 2], F32)
        CW = T("CW", [128, NL, 3, 2], F32)
        STF = T("STF", [128, 2, 512], F32)
        STB = T("STB", [128, 4, 512], BF16)
        VST = T("VST", [128, 3, 8 * 65], BF16)
        GCB = SCR[:, :, 0:512]
        ST6 = T("ST6", [128, 4, 2, 6], F32)
        MV = T("MV", [128, 4, 4], F32)
        RCP = T("RCP", [128, 2, 8], F32)
        ZERO = T("ZERO", [128, 16], F32)
        PSUM = st.enter_context(nc.psum_tensor("PSUM", [128, 8, 512], F32))

        rX = [[Res() for _ in range(2)] for _ in range(4)]
        rXM = [Res() for _ in range(2)]
        rXT = [Res() for _ in range(4)]
        rYT = [[Res() for _ in range(4)] for _ in range(8)]
        rGT = [Res() for _ in range(NFC)]
        rSIL = [Res() for _ in range(2)]
        rRING = [Res() for _ in range(NSLOT)]
        rLNP = [[Res(), Res()] for _ in range(2)]
        rKTW, rVAW, rBT = Res(), Res(), Res()
        rQM = [Res(), Res()]
        rPT = [Res() for _ in range(2)]
        rYC = [Res() for _ in range(2)]
        rID = Res()
        rSH = [Res() for _ in range(4)]
        rUB, rZB, rICN, rGBB = rSH
        rSA, rSB = Res(), Res()
        rCY = [Res(), Res()]
        rPTP = [Res() for _ in range(2)]
        rPW = [Res() for _ in range(2)]
        rSMALL = Res()
        rSTF = [Res() for _ in range(2)]
        rSTB = [Res() for _ in range(4)]
        rVST = [Res() for _ in range(3)]
        rGCB = [rSA, rSB]
        rST = [Res() for _ in range(4)]
        rST6 = [[Res(), Res()] for _ in range(4)]
        rRCP = [Res() for _ in range(2)]
        rZERO = Res()
        rBANK = [Res() for _ in range(8)]
        rX1s = [Res() for _ in range(NB)]
        rUTs = [[[Res() for _ in range(2)] for _ in range(NB)] for _ in range(2)]
        rZTs = [[[Res() for _ in range(2)] for _ in range(NB)] for _ in range(2)]
        rGBTs = [[Res() for _ in range(2)] for _ in range(NB)]
        rQTs = [[Res() for _ in range(4)] for _ in range(NB)]
        rKTs = [[[Res() for _ in range(4)] for _ in range(NB)] for _ in range(2)]
        rVAs = [[[Res() for _ in range(4)] for _ in range(NB)] for _ in range(2)]
        rPAD = Res()
        rWbf = {}

        state = {"bank": 0, "stf": 0, "stb": 0, "vst": 0, "sil": 0, "lnp": 0, "slot": 0, "xm": 0, "bst": 0,
                 "pt": 0, "yc": 0, "st": 0, "rcp": 0, "gcb": 0, "ptp": 0}

        def rot(key, n):
            i = state[key]
            state[key] = (i + 1) % n
            return i

        pinned = set()

        def bank_i():
            while True:
                i = rot("bank", 6)
                if i not in pinned:
                    return PSUM[:, i, :], rBANK[i], i

        def bank():
            b_, r_, _ = bank_i()
            return b_, r_

        deferred_tr = []

        def flush_deferred():
            for t_ in deferred_tr:
                transpose_tile(t_)
            del deferred_tr[:]

        def mm(o, lhsT, rhs, start, stop, reads, writes, skip=False):
            if skip:
                P.op("pe", lambda e: e.matmul(o, lhsT, rhs, start=start, stop=stop, skip_group_check=True), reads, writes)
            else:
                P.op("pe", lambda e: e.matmul(o, lhsT, rhs, start=start, stop=stop), reads, writes)

        P.op("pool", lambda e: e.memset(IDF[:], 0.0), [], [rID])
        P.op("pool", lambda e: e.affine_select(out=IDF[:], in_=IDF[:], pattern=[[-1, 128]], compare_op=ALU.not_equal,
                                               fill=1.0, base=0, channel_multiplier=1), [rID], [rID])
        P.op("pool", lambda e: e.tensor_copy(out=IDB[:], in_=IDF[:]), [rID], [rID])
        P.op("pool", lambda e: e.memset(ZERO[:], 0.0), [], [rZERO])
        P.op("pool", lambda e: e.memset(VST[:], 1.0), [], rVST)
        P.op("pool", lambda e: e.memset(PW[:], 0.0), [], [rPW[0], rPW[1]])
        P.op("pool", lambda e: e.memset(QM[:], 0.0), [], rQM)
        for i in range(2):
            for c in range(2):
                rows = slice(c * 128, (c + 1) * 128)
                P.dma("sp", UTs[i][rows, 0:8], ZERO[:, 0:8], [rZERO], [rPAD])
                P.dma("sp", UTs[i][rows, NTOK + 8:NTOK + 16], ZERO[:, 0:8], [rZERO], [rPAD])
                P.dma("sp", ZTs[i][rows, 0:1], ZERO[:, 0:1], [rZERO], [rPAD], slow=True)
                P.dma("sp", ZTs[i][rows, NTOK + 1:NTOK + 2], ZERO[:, 0:1], [rZERO], [rPAD], slow=True)
        for l in range(NL):
            P.dma("sp", PSC[:, l, :], pool_scale_in[l, :].rearrange("(c p) -> p c", p=128), [], [rSMALL], slow=True)
            for j in range(3):
                P.dma("sp", CW[:, l, j, :], conv_w_in[l, j, :].rearrange("(c p) -> p c", p=128), [], [rSMALL], slow=True)

        def panel_list(l):
            out_ = []
            def ffn(g, u, d):
                for pn in range(6):
                    out_.append((g, "col", pn)); out_.append((u, "col", pn))
                for h in range(2):
                    for kp in range(3):
                        out_.append((d, "blk", (kp, h)))
            return out_, ffn

        def conv_src_dst(key, l, kind, idx):
            src, dst = w_f32[key], w_bf[key]
            if kind == "col":
                c0 = idx * 512
                w = min(512, src.shape[2] - c0)
                return src[l, :, c0:c0 + w], dst[l, :, c0:c0 + w]
            kp, h = idx
            r0 = kp * 1024
            nr = min(1024, src.shape[1] - r0)
            return src[l, r0:r0 + nr, h * 512:(h + 1) * 512], dst[l, r0:r0 + nr, h * 512:(h + 1) * 512]

        conv_queue = []

        def queue_layer_conversions(l):
            pieces = []
            for pn in range(6):
                pieces += [("g1", l, "col", pn), ("u1", l, "col", pn)]
            for h in range(2):
                for kp in range(3):
                    pieces.append(("d1", l, "blk", (kp, h)))
            for pn in range(5):
                pieces.append(("win", l, "col", pn))
            for h in range(2):
                pieces.append(("wout", l, "col", h))
            for pn in range(6):
                pieces += [("g2", l, "col", pn), ("u2", l, "col", pn)]
            for h in range(2):
                for kp in range(3):
                    pieces.append(("d2", l, "blk", (kp, h)))
            conv_queue.extend(pieces)

        def pump_conversions(n):
            for _ in range(min(n, len(conv_queue))):
                key, l, kind, idx = conv_queue.pop(0)
                s, d = conv_src_dst(key, l, kind, idx)
                r = Res()
                rWbf[(key, l, kind, idx)] = r
                P.dma("pool", d, s, [], [r])

        def panel_sequence():
            seq = []
            def ffn_p(l, g, u, d):
                for pn in range(6):
                    seq.append((g, l, "col", pn)); seq.append((u, l, "col", pn))
                for h in range(2):
                    for kp in range(3):
                        seq.append((d, l, "blk", (kp, h)))
            for it in range(NL + 1):
                for b_ in range(NB):
                    if it >= 1:
                        seq.append(("wout", it - 1, "col", 0)); seq.append(("wout", it - 1, "col", 1))
                        ffn_p(it - 1, "g2", "u2", "d2")
                    if it < NL:
                        ffn_p(it, "g1", "u1", "d1")
                        for pn in range(5):
                            seq.append(("win", it, "col", pn))
            return seq

        pseq = panel_sequence()
        pst = {"issued": 0, "acq": 0}
        released = [False] * len(pseq)
        pinfo = {}

        def issue_panel(n):
            key, l, kind, idx = pseq[n]
            while (key, l, kind, idx) not in rWbf:
                pump_conversions(1)
            s = n % NSLOT
            src = w_bf[key]
            if kind == "col":
                c0 = idx * 512
                w = min(512, src.shape[2] - c0)
                sap = src[l, :, c0:c0 + w].rearrange("(k p) f -> p k f", p=128)
                dap = RING[:, s, 0:8 * w].rearrange("p (k f) -> p k f", k=8)
                P.dma("sp", dap, sap, [rWbf[(key, l, kind, idx)]], [rRING[s]])
                pinfo[n] = (s, w)
            else:
                kp, h = idx
                r0 = kp * 1024
                nr = min(1024, src.shape[1] - r0)
                nk = nr // 128
                sap = src[l, r0:r0 + nr, h * 512:(h + 1) * 512].rearrange("(k p) f -> p k f", p=128)
                dap = RING[:, s, 0:nk * 512].rearrange("p (k f) -> p k f", k=nk)
                P.dma("sp", dap, sap, [rWbf[(key, l, kind, idx)]], [rRING[s]])
                pinfo[n] = (s, nk)

        def pump_panels():
            while pst["issued"] < len(pseq) and (pst["issued"] < NSLOT or released[pst["issued"] - NSLOT]):
                issue_panel(pst["issued"])
                pst["issued"] += 1

        def load_panel(key, l, kind, idx):
            n = pst["acq"]
            assert pseq[n] == (key, l, kind, idx), (pseq[n], key, l, kind, idx)
            pump_panels()
            assert n < pst["issued"]
            pst["acq"] += 1
            s, w = pinfo[n]
            return n, s, w

        def release(*ns):
            for n in ns:
                released[n] = True
            pump_panels()

        def pview(s, w):
            return RING[:, s, 0:8 * w].rearrange("p (k f) -> p k f", k=8)

        def load_lnp(l, idx):
            i = rot("lnp", 2)
            P.dma("sp", LNP[:, i, 0, :], ln_g_in[l, idx, :].partition_broadcast(128), [], [rLNP[i][0]])
            P.dma("sp", LNP[:, i, 1, :], ln_b_in[l, idx, :].partition_broadcast(128), [], [rLNP[i][1]])
            return i

        def stats_half(t, h):
            P.op("dve", lambda e: e.bn_stats(out=ST6[:, t, h, :], in_=X[:, t, h * 512:(h + 1) * 512]), [rX[t][h]], [rST6[t][h]])

        def ln_tile(t, lnp_i, eps):
            si = t
            P.op("dve", lambda e: e.bn_aggr(out=MV[:, si, 0:2], in_=ST6[:, si, :, :].rearrange("p a b -> p (a b)")),
                 rST6[si], [rST[si]])
            P.op("act", lambda e: e.activation(out=MV[:, si, 2:3], in_=MV[:, si, 1:2], func=AF.Sqrt, bias=float(eps), scale=1.0),
                 [rST[si]], [rST[si]])
            P.op("dve", lambda e: e.reciprocal(out=MV[:, si, 2:3], in_=MV[:, si, 2:3]), [rST[si]], [rST[si]])
            for h in range(2):
                hs = slice(h * 512, (h + 1) * 512)
                P.op("dve", lambda e, hs=hs: e.scalar_tensor_tensor(out=X[:, t, hs], in0=X[:, t, hs], scalar=MV[:, si, 0:1],
                                                                    in1=LNP[:, lnp_i, 0, hs], op0=ALU.subtract, op1=ALU.mult),
                     [rST[si], rX[t][h], rLNP[lnp_i][0]], [rX[t][h]])
                P.op("dve", lambda e, hs=hs: e.scalar_tensor_tensor(out=X[:, t, hs], in0=X[:, t, hs], scalar=MV[:, si, 2:3],
                                                                    in1=LNP[:, lnp_i, 1, hs], op0=ALU.mult, op1=ALU.add),
                     [rST[si], rX[t][h], rLNP[lnp_i][1]], [rX[t][h]])

        def transpose_tile(t):
            for g in range(2):
                xr = [rX[t][g]]
                bk, rb = bank()
                for k in range(4):
                    kk = g * 4 + k
                    P.op("pe", lambda e, kk=kk, k=k, bk=bk: e.transpose(bk[:, k * 128:(k + 1) * 128], X[:, t, kk * 128:(kk + 1) * 128], IDF[:]),
                         xr + [rID], [rb])
                dst = XT[:, g * 4:(g + 1) * 4, t * 128:(t + 1) * 128]
                srcv = bk.rearrange("p (a b) -> p a b", a=4)
                if g == 0:
                    P.op("dve", lambda e, dst=dst, srcv=srcv: e.tensor_copy(out=dst, in_=srcv), [rb], [rXT[t]])
                else:
                    P.op("act", lambda e, dst=dst, srcv=srcv: e.activation(out=dst, in_=srcv, func=AF.Copy), [rb], [rXT[t]])

        def ffn(l, gk, uk, dk, ln_idx, last):
            lnp_i = load_lnp(l, ln_idx)
            pending = None
            for pn in range(6):
                ng, sg, w = load_panel(gk, l, "col", pn)
                nu, su, _ = load_panel(uk, l, "col", pn)
                pg, pu = pview(sg, w), pview(su, w)
                nch = w // 128
                nsplit = 2 if pn == 0 else 0
                held = {}
                for c in range(nsplit):
                    bg_, rbg_, ig_ = bank_i()
                    pinned.add(ig_)
                    bu_, rbu_, iu_ = bank_i()
                    pinned.add(iu_)
                    held[c] = ((bg_, rbg_), (bu_, rbu_), (ig_, iu_))
                for half in range(2 if nsplit else 0):
                    if half == 1:
                        flush_deferred()
                    cols = slice(half * 256, (half + 1) * 256)
                    rxt = [rXT[2 * half], rXT[2 * half + 1]]
                    for c in range(nsplit):
                        (bg, rbg), (bu, rbu), _ = held[c]
                        for k in range(8):
                            mm(bg[:, cols], pg[:, k, c * 128:(c + 1) * 128], XT[:, k, cols], k == 0, k == 7, rxt + [rRING[sg]], [rbg])
                        for k in range(8):
                            mm(bu[:, cols], pu[:, k, c * 128:(c + 1) * 128], XT[:, k, cols], k == 0, k == 7, rxt + [rRING[su]], [rbu])
                for c in range(nch):
                    fc = pn * 4 + c
                    if c < nsplit:
                        (bg, rbg), (bu, rbu), idxs = held[c]
                        pinned.discard(idxs[0])
                        pinned.discard(idxs[1])
                    else:
                        bg, rbg = bank()
                        bu, rbu = bank()
                        for k in range(8):
                            mm(bg, pg[:, k, c * 128:(c + 1) * 128], XT[:, k, :], k == 0, k == 7, rXT + [rRING[sg]], [rbg])
                        for k in range(8):
                            mm(bu, pu[:, k, c * 128:(c + 1) * 128], XT[:, k, :], k == 0, k == 7, rXT + [rRING[su]], [rbu])
                    si = rot("sil", 2)
                    P.op("act", lambda e, si=si, bg=bg: e.activation(out=SIL[:, si, :], in_=bg, func=AF.Silu), [rbg], [rSIL[si]])
                    P.op("dve", lambda e, si=si, bu=bu, fc=fc: e.tensor_tensor(out=GT[:, fc, :], in0=SIL[:, si, :], in1=bu, op=ALU.mult),
                         [rSIL[si], rbu], [rGT[fc]])
                release(ng, nu)
            for h in range(2):
                slots = [load_panel(dk, l, "blk", (kp, h)) for kp in range(3)]
                for t in range(4):
                    by, rby = bank()
                    for fc in range(NFC):
                        _, s, nk = slots[fc // 8]
                        pv = RING[:, s, 0:nk * 512].rearrange("p (k f) -> p k f", k=nk)
                        mm(by, GT[:, fc, t * 128:(t + 1) * 128], pv[:, fc % 8, :], fc == 0, fc == NFC - 1,
                           [rGT[fc], rRING[s]], [rby])
                    P.op("dve", lambda e, t=t, h=h, by=by: e.scalar_tensor_tensor(
                        out=X[:, t, h * 512:(h + 1) * 512], in0=X[:, t, h * 512:(h + 1) * 512], scalar=2.0 * ALPHA,
                        in1=by, op0=ALU.mult, op1=ALU.add), [rX[t][h], rby], [rX[t][h]])
                    stats_half(t, h)
                    if h == 1:
                        ln_tile(t, lnp_i, 4.0 * LN_EPS)
                        if not last and t in (1, 2):
                            transpose_tile(t - 1)
                release(*[x[0] for x in slots])
            if not last:
                deferred_tr.extend([2, 3])

        def alt_view(k):
            return GT[:, k * 10:(k + 1) * 10, :].rearrange("p a b -> p (a b)").rearrange("p (i f) -> p i f", i=5)

        ALT = [alt_view(0), alt_view(1)]
        rALT = [rGT[0:10], rGT[10:20]]

        def prefetch_tables(l, b):
            if b == 0:
                P.dma("pool", BT[:], biasg_in[l, 0], [], [rBT])
                P.dma("pool", ALT[0], biasg_in[l, 1], [], rALT[0])
                P.dma("pool", ALT[1], biasg_in[l, 2], [], rALT[1])
            if b == NB - 1:
                P.dma("pool", ALT[0], biasg_in[l, 3], [], rALT[0])
                P.dma("pool", ALT[1], biasg_in[l, 4], [], rALT[1])


        def mixer_window(b):
            w0 = min(max(4 * b - 2, 0), W0_MAX)
            wblocks = sorted(set(min(max((w0 + i) // 4, 0), NB - 1) for i in range(8)))
            return w0, wblocks

        def mixer_chunk_loads(l, b, c):
            par = l % 2
            b0 = b * 512
            nbr = [bb for bb in (b - 1, b, b + 1) if 0 <= bb < NB]
            rows = slice(c * 128, (c + 1) * 128)
            P.dma("sp", UB[:], UTs[par][rows, b0:b0 + 528], [rUTs[par][bb][c] for bb in nbr] + [rPAD], [rUB])
            P.dma("sp", ICN[:], invcnt_in[c, :, b0:b0 + 512], [], [rICN])
            P.dma("sp", ZB[:], ZTs[par][rows, b0:b0 + 514], [rZTs[par][bb][c] for bb in nbr] + [rPAD], [rZB])
            P.dma("sp", GBB[:], GBTs[rows, b0:b0 + 512], [rGBTs[b][c]], [rGBB])

        def mixer_prefetch(l, b):
            par = l % 2
            b0 = b * 512
            w0, wblocks = mixer_window(b)
            mixer_chunk_loads(l, b, 0)
            qsrc = QTs[:, b0:b0 + 512].rearrange("(c p) t -> p c t", p=128)
            P.dma("sp", QM[0:64, 0, :, :], qsrc[0:64], rQTs[b], [rQM[0]])
            P.dma("sp", QM[64:128, 1, :, :], qsrc[64:128], rQTs[b], [rQM[1]])
            P.dma("sp", KTW[:], KTs[par][:, w0 * 128:w0 * 128 + 1024].rearrange("(c p) t -> p c t", p=128),
                  [r for bb in wblocks for r in rKTs[par][bb]], [rKTW])
            P.dma("sp", VAW[:], VAs[par][w0 * 128:w0 * 128 + 1024, :].rearrange("(t p) f -> p t f", p=128),
                  [r for bb in wblocks for r in rVAs[par][bb]], [rVAW])

        def mixer(l, b, prefetched):
            par = l % 2
            b0 = b * 512
            w0, wblocks = mixer_window(b)
            nbr = [bb for bb in (b - 1, b, b + 1) if 0 <= bb < NB]
            if b == 0:
                for g in range(4):
                    c, hh = g // 2, g % 2
                    P.dma("pool", PW[hh * 64:(hh + 1) * 64, par, c, hh * 64:(hh + 1) * 64], pool_w_in[l, g, :, :], [], [rPW[par]])
            if not prefetched:
                mixer_prefetch(l, b)
            lnp_i = load_lnp(l, 1)
            lo, hi = slice(0, 64), slice(64, 128)

            def add(o, a_, b_, rr, ww):
                P.op("pool", lambda e: e.tensor_tensor(out=o, in0=a_, in1=b_, op=ALU.add), rr, ww)

            def elementwise(c):
                add(SA[:, 1:528], UB[:, 1:528], UB[:, 0:527], [rUB], [rSA])
                if c == 0:
                    add(SB[hi, 3:528], SA[hi, 3:528], SA[hi, 1:526], [rSA], [rSB])
                    sel_lo, sel_hi = SA[lo, 8:520], SB[hi, 9:521]
                else:
                    add(SB[:, 3:528], SA[:, 3:528], SA[:, 1:526], [rSA], [rSB])
                    add(SA[:, 7:528], SB[:, 7:528], SB[:, 3:524], [rSB], [rSA])
                    add(SB[hi, 15:528], SA[hi, 15:528], SA[hi, 7:520], [rSA], [rSB])
                    sel_lo, sel_hi = SA[lo, 11:523], SB[hi, 15:527]
                for part, sel in ((lo, sel_lo), (hi, sel_hi)):
                    P.op("pool", lambda e, part=part, sel=sel: e.tensor_tensor(out=CY[part, 0, :], in0=sel, in1=ICN[part, :], op=ALU.mult),
                         [rSA, rSB, rICN], [rCY[0]])
                    P.op("pool", lambda e, part=part, c=c: e.tensor_tensor(out=PTP[part, c, :], in0=CY[part, 0, :], in1=UB[part, 8:520], op=ALU.subtract),
                         [rCY[0], rUB], [rPTP[c]])
                P.op("dve", lambda e, c=c: e.tensor_scalar(out=CY[:, 1, :], in0=ZB[:, 1:513], scalar1=CW[:, l, 1, c:c + 1], scalar2=None, op0=ALU.mult),
                     [rZB, rSMALL], [rCY[1]])
                P.op("dve", lambda e, c=c: e.scalar_tensor_tensor(out=CY[:, 1, :], in0=ZB[:, 0:512], scalar=CW[:, l, 0, c:c + 1], in1=CY[:, 1, :],
                                                                  op0=ALU.mult, op1=ALU.add), [rZB, rSMALL, rCY[1]], [rCY[1]])
                P.op("dve", lambda e, c=c: e.scalar_tensor_tensor(out=CY[:, 1, :], in0=ZB[:, 2:514], scalar=CW[:, l, 2, c:c + 1], in1=CY[:, 1, :],
                                                                  op0=ALU.mult, op1=ALU.add), [rZB, rSMALL, rCY[1]], [rCY[1]])
                P.op("dve", lambda e, c=c: e.tensor_tensor(out=YT[:, 2 + c, :], in0=CY[:, 1, :], in1=GBB[:], op=ALU.mult),
                     [rCY[1], rGBB], rYT[2 + c])

            elementwise(0)
            mixer_chunk_loads(l, b, 1)

            ob = [(PSUM[:, 6, :], rBANK[6]), (PSUM[:, 7, :], rBANK[7])]
            tile_ctx = {}

            def scores(t, hp):
                j = 4 * b + t
                if hp == 0:
                    ks = min(max(j - 2, 0), KS_MAX)
                    tile_ctx[t] = (ks - w0, rot("yc", 2), rot("rcp", 2))
                kw, yi, ri = tile_ctx[t]
                tid = {0: 1, 1: 2, NT - 2: 3, NT - 1: 4}.get(j, 0)
                if tid == 0:
                    BTv, rBTv = BT, [rBT]
                else:
                    BTv, rBTv = ALT[(tid - 1) % 2], rALT[(tid - 1) % 2]
                pi = rot("pt", 2)
                bA, rA = bank()
                bB, rB = bank()
                bC, rC = bank()
                hsl = slice(2 * hp * 128, (2 * hp + 2) * 128)
                mm(bA.rearrange("p (a b) -> p a b", a=2), IDB[:], BTv[:, 0:2, hsl], True, False, [rID] + rBTv, [rA], skip=True)
                mm(bB.rearrange("p (a b) -> p a b", a=2), IDB[:], BTv[:, 2:4, hsl], True, False, [rID] + rBTv, [rB], skip=True)
                mm(bC[:, 0:256], IDB[:], BTv[:, 4, hsl], True, False, [rID] + rBTv, [rC], skip=True)
                for i in range(5):
                    bk_, rr = ((bA, rA), (bB, rB), (bC, rC))[i // 2]
                    o = bk_[:, (i % 2) * 256:(i % 2 + 1) * 256].rearrange("p (a b) -> p a b", a=2)
                    mm(o, KTW[:, hp, (kw + i) * 128:(kw + i + 1) * 128], QM[:, :, hp, t * 128:(t + 1) * 128],
                       False, True, [rKTW] + rQM, [rr], skip=True)
                P.op("act", lambda e: e.activation(out=PT[:, pi, 0:512], in_=bA, func=AF.Exp), [rA], [rPT[pi]])
                P.op("act", lambda e: e.activation(out=PT[:, pi, 512:1024], in_=bB, func=AF.Exp), [rB], [rPT[pi]])
                P.op("act", lambda e: e.activation(out=PT[:, pi, 1024:1280], in_=bC[:, 0:256], func=AF.Exp), [rC], [rPT[pi]])

                def pv():
                    for hh in range(2):
                        h = 2 * hp + hh
                        obk, obr = ob[h // 4]
                        o = obk[:, (h % 4) * 65:(h % 4 + 1) * 65]
                        for i in range(5):
                            c0 = i * 256 + hh * 128
                            mm(o, PT[:, pi, c0:c0 + 128], VAW[:, kw + i, h * 65:(h + 1) * 65], i == 0, i == 4,
                               [rPT[pi], rVAW], [obr])
                    if hp % 2 == 1:
                        g = hp // 2
                        obk, obr = ob[g]
                        ov = obk[:, 0:260].rearrange("p (h d) -> p h d", h=4)
                        P.op("dve", lambda e: e.reciprocal(out=RCP[:, ri, g * 4:(g + 1) * 4], in_=ov[:, :, 64]), [obr], [rRCP[ri]])
                        for h4 in range(4):
                            h = g * 4 + h4
                            P.op("dve", lambda e, h4=h4, h=h: e.tensor_scalar(
                                out=YC[:, yi, h * 64:(h + 1) * 64], in0=ov[:, h4, 0:64], scalar1=RCP[:, ri, h:h + 1], scalar2=None, op0=ALU.mult),
                                [obr, rRCP[ri]], [rYC[yi]])
                    if hp != 3:
                        return None

                    def tr():
                        bk, rb = bank()
                        bkb = bk.bitcast(BF16)
                        for c in range(4):
                            P.op("pe", lambda e, c=c: e.transpose(bkb[:, c * 128:(c + 1) * 128], YC[:, yi, c * 128:(c + 1) * 128], IDB[:]),
                                 [rYC[yi], rID], [rb])
                        P.op("act", lambda e: e.activation(out=YT[:, 4:8, t * 128:(t + 1) * 128],
                                                           in_=bkb[:, 0:512].rearrange("p (a b) -> p a b", a=4), func=AF.Copy),
                             [rb], [rYT[4 + c][t] for c in range(4)])
                    return tr
                return pv

            queue = []
            n = 0
            for t in range(4):
                for hp in range(4):
                    pvf = scores(t, hp)
                    due = [f for (d, f) in queue if d <= n]
                    queue = [(d, f) for (d, f) in queue if d > n]
                    for f in due:
                        r_ = f()
                        if r_ is not None:
                            queue.append((n + 1, r_))
                    queue.append((n + 1, pvf))
                    n += 1
                if t == 1:
                    elementwise(1)
            while queue:
                d, f = queue.pop(0)
                r_ = f()
                if r_ is not None:
                    queue.append((d + 1, r_))

            for c in range(2):
                bk, rb = bank()
                mm(bk, PW[:, par, c, :], PTP[:, c, :], True, True, [rPW[par], rPTP[c]], [rb])
                P.op("act", lambda e, c=c, bk=bk: e.activation(out=YT[:, c, :], in_=bk, func=AF.Copy, scale=PSC[:, l, c:c + 1]),
                     [rb, rSMALL], rYT[c])

            panels = [load_panel("wout", l, "col", h) for h in range(2)]
            for t in range(4):
                xi = rot("xm", 2)
                P.dma("sp", XM[:, xi, :], X1s[b0 + t * 128:b0 + (t + 1) * 128, :], [rX1s[b]], [rXM[xi]])
                for h in range(2):
                    _, s_, w = panels[h]
                    pv_ = pview(s_, w)
                    by, rby = bank()
                    for k in range(8):
                        mm(by, YT[:, k, t * 128:(t + 1) * 128], pv_[:, k, :], k == 0, k == 7,
                           [rYT[k][tt] for tt in range(4)] + [rRING[s_]], [rby])
                    P.op("dve", lambda e, t=t, h=h, by=by, xi=xi: e.scalar_tensor_tensor(
                        out=X[:, t, h * 512:(h + 1) * 512], in0=XM[:, xi, h * 512:(h + 1) * 512], scalar=ALPHA,
                        in1=by, op0=ALU.mult, op1=ALU.add), [rXM[xi], rby], [rX[t][h]])
                    stats_half(t, h)
                ln_tile(t, lnp_i, LN_EPS)
                if t in (1, 2):
                    transpose_tile(t - 1)
            release(panels[0][0], panels[1][0])
            deferred_tr.extend([2, 3])

        SQ = "act"

        def project(l, b):
            par = l % 2
            b0 = b * 512
            P.dma(SQ, X1s[b0:b0 + 512, :].rearrange("(t p) d -> p t d", p=128), X[:],
                  [rX[t][h] for t in range(4) for h in range(2)], [rX1s[b]])
            for pn in range(4):
                n_, s, w = load_panel("win", l, "col", pn)
                pv = pview(s, w)
                heldp = {}
                if pn == 0:
                    for c in range(4):
                        bk_, rb_, i_ = bank_i()
                        pinned.add(i_)
                        heldp[c] = (bk_, rb_, i_)
                    for half in range(2):
                        if half == 1:
                            flush_deferred()
                        cols = slice(half * 256, (half + 1) * 256)
                        rxt = [rXT[2 * half], rXT[2 * half + 1]]
                        for c in range(4):
                            bk_, rb_, _ = heldp[c]
                            for k in range(8):
                                mm(bk_[:, cols], pv[:, k, c * 128:(c + 1) * 128], XT[:, k, cols], k == 0, k == 7, rxt + [rRING[s]], [rb_])
                for c in range(4):
                    if c in heldp:
                        bk, rb, i_ = heldp[c]
                        pinned.discard(i_)
                    else:
                        bk, rb = bank()
                        for k in range(8):
                            mm(bk, pv[:, k, c * 128:(c + 1) * 128], XT[:, k, :], k == 0, k == 7, rXT + [rRING[s]], [rb])
                    if pn == 0:
                        i = rot("stf", 2)
                        if c < 2:
                            P.op("act", lambda e, i=i, bk=bk: e.activation(out=STF[:, i, :], in_=bk, func=AF.Copy), [rb], [rSTF[i]])
                            P.dma(SQ, UTs[par][c * 128:(c + 1) * 128, 8 + b0:8 + b0 + 512], STF[:, i, :], [rSTF[i]], [rUTs[par][b][c]])
                        else:
                            P.op("dve", lambda e, i=i, bk=bk: e.tensor_copy(out=STF[:, i, :], in_=bk), [rb], [rSTF[i]])
                            P.dma(SQ, GBTs[(c - 2) * 128:(c - 1) * 128, b0:b0 + 512], STF[:, i, :], [rSTF[i]], [rGBTs[b][c - 2]])
                    elif pn == 1:
                        if c < 2:
                            gi = c
                            P.op("act", lambda e, gi=gi, bk=bk: e.activation(out=GCB[:, gi, :], in_=bk, func=AF.Copy), [rb], [rGCB[gi]])
                        else:
                            gi = c - 2
                            i = rot("stf", 2)
                            P.op("dve", lambda e, i=i, gi=gi, bk=bk: e.tensor_tensor(out=STF[:, i, :], in0=GCB[:, gi, :], in1=bk, op=ALU.mult),
                                 [rb, rGCB[gi]], [rSTF[i]])
                            P.dma(SQ, ZTs[par][gi * 128:(gi + 1) * 128, 1 + b0:1 + b0 + 512], STF[:, i, :], [rSTF[i]], [rZTs[par][b][gi]])
                    elif pn == 2:
                        i = rot("stb", 4)
                        P.op("act", lambda e, i=i, bk=bk: e.activation(out=STB[:, i, :], in_=bk, func=AF.Copy, scale=0.125), [rb], [rSTB[i]])
                        P.dma(SQ, QTs[c * 128:(c + 1) * 128, b0:b0 + 512], STB[:, i, :], [rSTB[i]], [rQTs[b][c]])
                    else:
                        i = rot("stb", 4)
                        P.op("dve", lambda e, i=i, bk=bk: e.tensor_copy(out=STB[:, i, :], in_=bk), [rb], [rSTB[i]])
                        P.dma(SQ, KTs[par][c * 128:(c + 1) * 128, b0:b0 + 512], STB[:, i, :], [rSTB[i]], [rKTs[par][b][c]])
                release(n_)
            n_, s, w = load_panel("win", l, "col", 4)
            pv = pview(s, w)
            for t in range(4):
                bk, rb = bank()
                for k in range(8):
                    mm(bk, XT[:, k, t * 128:(t + 1) * 128], pv[:, k, :], k == 0, k == 7, [rXT[t], rRING[s]], [rb])
                vi = rot("vst", 3)
                dst = VST[:, vi, :].rearrange("p (h d) -> p h d", h=8)[:, :, 0:64]
                srcv = bk.rearrange("p (h d) -> p h d", h=8)
                if t % 2 == 0:
                    P.op("act", lambda e, dst=dst, srcv=srcv: e.activation(out=dst, in_=srcv, func=AF.Copy), [rb], [rVST[vi]])
                else:
                    P.op("dve", lambda e, dst=dst, srcv=srcv: e.tensor_copy(out=dst, in_=srcv), [rb], [rVST[vi]])
                P.dma(SQ, VAs[par][b0 + t * 128:b0 + (t + 1) * 128, :], VST[:, vi, :], [rVST[vi]], [rVAs[par][b][t]])
            release(n_)

        queue_layer_conversions(0)
        pump_conversions(len(conv_queue))
        out_stores = []
        for it in range(NL + 1):
            if it + 1 < NL:
                queue_layer_conversions(it + 1)
            per_block = (len(conv_queue) + NB - 1) // NB if conv_queue else 0
            for b in range(NB):
                b0 = b * 512
                if it >= 1:
                    mixer(it - 1, b, prefetched=(b >= 1))
                    ffn(it - 1, "g2", "u2", "d2", 2, last=(it == NL))
                else:
                    P.dma("sp", X[:], x_in[b0:b0 + 512, :].rearrange("(t p) d -> p t d", p=128), [],
                          [rX[t][h] for t in range(4) for h in range(2)])
                    for t in range(4):
                        transpose_tile(t)
                if it == NL:
                    out_stores.append(P.dma("sp", out[b0:b0 + 512, :].rearrange("(t p) d -> p t d", p=128), X[:],
                                            [rX[t][h] for t in range(4) for h in range(2)], []))
                    if b + 1 < NB:
                        mixer_prefetch(it - 1, b + 1)
                        prefetch_tables(it - 1, b + 1)
                else:
                    if it >= 1 and b + 1 < NB:
                        mixer_prefetch(it - 1, b + 1)
                    ffn(it, "g1", "u1", "d1", 0, last=False)
                    if b + 1 < NB:
                        if it >= 1:
                            prefetch_tables(it - 1, b + 1)
                    else:
                        prefetch_tables(it, 0)
                    project(it, b)
                pump_conversions(per_block)
        P.emit(final_waits=out_stores)
    return nc


def _tables(NT):
    R = 2 * NT
    KS_MAX = NT - 5
    tiles = [min(10, NT // 2), 0, 1, NT - 2, NT - 1]
    dr_idx = np.zeros((5, 128, 5, 128), np.int64)
    dc_idx = np.zeros((5, 128, 5, 128), np.int64)
    valid = np.zeros((5, 128, 5, 128), bool)
    kk = np.arange(128)[:, None]
    qq = np.arange(128)[None, :]
    for ti, j in enumerate(tiles):
        ks = min(max(j - 2, 0), KS_MAX)
        for i in range(5):
            kt = ks + i
            krow = 2 * kt + kk // 64
            kcol = kk % 64
            qrow = 2 * j + qq // 64
            qcol = qq % 64
            rs = np.clip(qrow - 4, 0, R - 8)
            cs = np.clip(qcol - 8, 0, GRID_W - 16)
            v = (krow >= rs) & (krow < rs + 8) & (kcol >= cs) & (kcol < cs + 16)
            dr = np.clip(krow - qrow + 7, 0, 14)
            dc = np.clip(kcol - qcol, -15, 15) + 15
            dr_idx[ti, :, i, :] = np.broadcast_to(dr, (128, 128))
            dc_idx[ti, :, i, :] = np.broadcast_to(dc, (128, 128))
            valid[ti, :, i, :] = v
    return dr_idx, dc_idx, valid


def _invcnt(NTOK, off, S):
    t = np.arange(NTOK) + off
    res = np.zeros((2, 128, NTOK), np.float32)
    for g, w in enumerate((2, 4, 8, 16)):
        lo = np.clip(t - w // 2, 0, S)
        hi = np.clip(t - w // 2 + w, 0, S)
        cnt = np.maximum(hi - lo, 1).astype(np.float32)
        c, hh = g // 2, g % 2
        res[c, hh * 64:(hh + 1) * 64, :] = (1.0 / cnt)[None, :]
    return res


_CACHE = {}


def run_cores(xs, offs, S, params, NL, NB):
    NT = 4 * NB
    NTOK = NT * 128
    key = (NL, NB)
    if key not in _CACHE:
        _CACHE[key] = build_program(NL, NB)
    nc = _CACHE[key]
    dr_idx, dc_idx, valid = _tables(NT)
    rpb = np.asarray(params["rpb"], np.float32)[:NL]
    rpb_ext = np.concatenate([rpb, np.full((NL, 8, 1, 31), NEG, np.float32)], axis=2)
    dr_m = np.where(valid, dr_idx, 15)
    g = rpb_ext[:, :, dr_m, dc_idx]
    bias_g = np.ascontiguousarray(np.transpose(g, (0, 2, 3, 4, 1, 5))).reshape(NL, 5, 128, 5, 8 * 128)
    common = {
        "bias_g": bias_g,
        "pool_w": np.ascontiguousarray(params["pool_w"][:NL], np.float32),
        "pool_scale": np.ascontiguousarray(params["pool_scale"][:NL], np.float32),
        "conv_w": np.ascontiguousarray(params["conv_w"][:NL], np.float32),
        "ln_g": np.ascontiguousarray(params["ln_g"][:NL], np.float32),
        "ln_b": np.ascontiguousarray(params["ln_b"][:NL], np.float32),
    }
    for k in ("ffn1_w_gate", "ffn1_w_up", "ffn1_w_down", "ffn2_w_gate", "ffn2_w_up", "ffn2_w_down", "w_in", "w_out"):
        common[k] = np.ascontiguousarray(params[k][:NL], np.float32)
    in_maps = []
    for xw, off in zip(xs, offs):
        m = dict(common)
        m["x"] = np.ascontiguousarray(xw, np.float32)
        m["invcnt"] = _invcnt(NTOK, off, S)
        in_maps.append(m)
    res = run_bass_kernel_spmd(nc, in_maps, core_ids=list(range(len(xs))))
    return [np.asarray(r["out"]) for r in res.results]


def kernel(**inputs):
    x = np.asarray(inputs["x"], np.float32)
    B, S, _ = x.shape
    NB = 10
    NTOK = NB * 512
    half = S // 2
    xs, offs = [], []
    for bi in range(B):
        xs.append(x[bi, 0:NTOK]); offs.append(0)
        xs.append(x[bi, S - NTOK:S]); offs.append(S - NTOK)
    outs = run_cores(xs, offs, S, inputs, 4, NB)
    y = np.empty((B, S, D), np.float32)
    for bi in range(B):
        y[bi, 0:half] = outs[2 * bi][0:half]
        y[bi, half:S] = outs[2 * bi + 1][NTOK - half:NTOK]
    return y
```
